# Optimizing a Trainium2 kernel written in Bass

```python
import jax, jax.numpy as jnp
from jax import lax
import numpy as np

D_MODEL = 4096
BATCH = 2
SEQ = 4096
DEPTH = 2

HEAD_DIM = 128
D_LRU = D_MODEL // 2
LRU_HEADS = D_LRU // HEAD_DIM
D_DN = D_MODEL // 2
DN_HEADS = D_DN // HEAD_DIM
D_SC = D_MODEL
LRU_CONV_W = 4
DN_CONV_W = 4
SC_CONV_W = 3
LRU_C = 8.0
CHUNK = 64
EPS = 1e-6
N_EVEN = (DEPTH + 1) // 2
N_ODD = DEPTH // 2
D_IN0 = 2 * D_LRU + 4 * D_DN + 2 * DN_HEADS
D_IN1 = 4 * D_SC

kernel_name = "hybrid_rglru_gdn_shortconv"


def rmsnorm(x, w):
    xf = x.astype(jnp.float32)
    xf = xf * lax.rsqrt(jnp.mean(xf * xf, axis=-1, keepdims=True) + EPS)
    return xf.astype(x.dtype) * w


def l2norm(x):
    return x * lax.rsqrt(jnp.sum(x * x, axis=-1, keepdims=True) + EPS)


def causal_dwconv(x, w):
    K = w.shape[0]
    T = x.shape[1]
    xp = jnp.pad(x, ((0, 0), (K - 1, 0), (0, 0)))
    return sum(xp[:, k:k + T] * w[k] for k in range(K))


def rg_lru(x, w_r, b_r, w_i, b_i, lam):
    Bn, T, _ = x.shape
    xh = x.reshape(Bn, T, LRU_HEADS, HEAD_DIM)
    r = jax.nn.sigmoid(jnp.einsum('bthi,hij->bthj', xh, w_r).reshape(Bn, T, D_LRU) + b_r)
    gi = jax.nn.sigmoid(jnp.einsum('bthi,hij->bthj', xh, w_i).reshape(Bn, T, D_LRU) + b_i)
    log_a = -LRU_C * r.astype(jnp.float32) * jax.nn.softplus(-lam.astype(jnp.float32))
    a = jnp.exp(log_a)
    u = jnp.sqrt(-jnp.expm1(2.0 * log_a)) * (gi * x).astype(jnp.float32)

    def combine(c1, c2):
        a1, b1 = c1
        a2, b2 = c2
        return a1 * a2, a2 * b1 + b2

    _, h = lax.associative_scan(combine, (a, u), axis=1)
    return h.astype(x.dtype)


def gated_delta_chunked(q, k, v, g, beta):
    Bn, T, H, Dk = q.shape
    Dv = v.shape[-1]
    N = T // CHUNK
    f32 = jnp.float32
    q = l2norm(q.astype(f32)) * (Dk ** -0.5)
    k = l2norm(k.astype(f32))
    v = v.astype(f32)

    def to_chunks(t):
        return t.reshape(Bn, N, CHUNK, H, -1).transpose(0, 3, 1, 2, 4)

    q, k, v = to_chunks(q), to_chunks(k), to_chunks(v)
    g = g.astype(f32).reshape(Bn, N, CHUNK, H).transpose(0, 3, 1, 2)
    beta = beta.astype(f32).reshape(Bn, N, CHUNK, H).transpose(0, 3, 1, 2)
    g_cum = jnp.cumsum(g, axis=-1)
    causal = jnp.tril(jnp.ones((CHUNK, CHUNK), dtype=bool))
    strict = jnp.tril(jnp.ones((CHUNK, CHUNK), dtype=bool), -1)
    decay = jnp.exp(jnp.where(causal, g_cum[..., :, None] - g_cum[..., None, :], -jnp.inf))
    k_beta = k * beta[..., None]
    v_beta = v * beta[..., None]
    a_mat = jnp.where(strict, jnp.einsum('bhncd,bhnsd->bhncs', k_beta, k) * decay, 0.0)
    lhs = a_mat + jnp.eye(CHUNK, dtype=f32)
    rhs = jnp.concatenate([v_beta, k_beta * jnp.exp(g_cum)[..., None]], axis=-1)
    sol = lax.linalg.triangular_solve(lhs, rhs, left_side=True, lower=True, unit_diagonal=True)
    u_c, w_c = sol[..., :Dv], sol[..., Dv:]
    attn = jnp.einsum('bhncd,bhnsd->bhncs', q, k) * decay
    q_dec = q * jnp.exp(g_cum)[..., None]
    k_dec = k * jnp.exp(g_cum[..., -1:] - g_cum)[..., None]
    g_last = jnp.exp(g_cum[..., -1])

    xs = (u_c.transpose(2, 0, 1, 3, 4), w_c.transpose(2, 0, 1, 3, 4),
          attn.transpose(2, 0, 1, 3, 4), q_dec.transpose(2, 0, 1, 3, 4),
          k_dec.transpose(2, 0, 1, 3, 4), g_last.transpose(2, 0, 1))

    def step(S, inp):
        u_i, w_i, attn_i, qd_i, kd_i, gl_i = inp
        v_new = u_i - jnp.einsum('bhcd,bhde->bhce', w_i, S)
        o_i = jnp.einsum('bhcd,bhde->bhce', qd_i, S) + jnp.einsum('bhcs,bhse->bhce', attn_i, v_new)
        S = S * gl_i[..., None, None] + jnp.einsum('bhcd,bhce->bhde', kd_i, v_new)
        return S, o_i

    S0 = jnp.zeros((Bn, H, Dk, Dv), f32)
    _, o = lax.scan(step, S0, xs)
    return o.transpose(1, 0, 3, 2, 4).reshape(Bn, T, H, Dv)


def lru_deltanet_layer(x, norm_w, w_in, lru_conv_w, lru_conv_b, lru_w_r, lru_b_r,
                       lru_w_i, lru_b_i, lru_lambda, dn_conv_w, dn_a_log, dn_dt_bias,
                       dn_norm_w, w_out):
    Bn, T, _ = x.shape
    h = rmsnorm(x, norm_w)
    proj = h @ w_in
    splits = [D_LRU, 2 * D_LRU, 2 * D_LRU + 3 * D_DN, 2 * D_LRU + 4 * D_DN,
              2 * D_LRU + 4 * D_DN + DN_HEADS]
    xa, ga, qkv, z, b_raw, a_raw = jnp.split(proj, splits, axis=-1)
    xa = causal_dwconv(xa, lru_conv_w) + lru_conv_b
    ya = rg_lru(xa, lru_w_r, lru_b_r, lru_w_i, lru_b_i, lru_lambda) * jax.nn.silu(ga)
    qkv = jax.nn.silu(causal_dwconv(qkv, dn_conv_w))
    q, k, v = jnp.split(qkv, 3, axis=-1)
    q = q.reshape(Bn, T, DN_HEADS, HEAD_DIM)
    k = k.reshape(Bn, T, DN_HEADS, HEAD_DIM)
    v = v.reshape(Bn, T, DN_HEADS, HEAD_DIM)
    beta = jax.nn.sigmoid(b_raw.astype(jnp.float32))
    g = -jnp.exp(dn_a_log.astype(jnp.float32)) * jax.nn.softplus(
        a_raw.astype(jnp.float32) + dn_dt_bias.astype(jnp.float32))
    o = gated_delta_chunked(q, k, v, g, beta).astype(x.dtype)
    o = rmsnorm(o, dn_norm_w) * jax.nn.silu(z.reshape(Bn, T, DN_HEADS, HEAD_DIM))
    yb = o.reshape(Bn, T, D_DN)
    y = jnp.concatenate([ya, yb], axis=-1) @ w_out
    return x + y


def shortconv_layer(x, norm_w, w_in, conv_w, w_out):
    h = rmsnorm(x, norm_w)
    xin, gate_b, gate_c, z = jnp.split(h @ w_in, 4, axis=-1)
    y = gate_b * causal_dwconv(gate_c * xin, conv_w) * jax.nn.silu(z)
    return x + y @ w_out


def setup_inputs(seed: int = 0) -> dict:
    key = jax.random.key(seed)
    ks = jax.random.split(key, 24)
    f32 = jnp.float32
    nrm = lambda k, shape, s: jax.random.normal(k, shape, f32) * s
    x = jax.random.normal(ks[0], (BATCH, SEQ, D_MODEL), f32)
    a0 = jax.random.uniform(ks[8], (N_EVEN, D_LRU), f32, 0.9, 0.999)
    a_root = a0 ** (1.0 / LRU_C)
    lru_lambda = jnp.log(a_root) - jnp.log1p(-a_root)
    dn_a_log = jnp.log(jax.random.uniform(ks[10], (N_EVEN, DN_HEADS), f32, 1.0, 16.0))
    dt = jnp.exp(jax.random.uniform(ks[11], (N_EVEN, DN_HEADS), f32,
                                    float(np.log(1e-3)), float(np.log(1e-1))))
    dn_dt_bias = dt + jnp.log(-jnp.expm1(-dt))
    return {
        "x": x,
        "even_norm_w": 1.0 + nrm(ks[1], (N_EVEN, D_MODEL), 0.02),
        "even_w_in": nrm(ks[2], (N_EVEN, D_MODEL, D_IN0), D_MODEL ** -0.5),
        "lru_conv_w": nrm(ks[3], (N_EVEN, LRU_CONV_W, D_LRU), LRU_CONV_W ** -0.5),
        "lru_conv_b": nrm(ks[4], (N_EVEN, D_LRU), 0.02),
        "lru_w_r": nrm(ks[5], (N_EVEN, LRU_HEADS, HEAD_DIM, HEAD_DIM), HEAD_DIM ** -0.5),
        "lru_b_r": nrm(ks[6], (N_EVEN, D_LRU), 0.02),
        "lru_w_i": nrm(ks[7], (N_EVEN, LRU_HEADS, HEAD_DIM, HEAD_DIM), HEAD_DIM ** -0.5),
        "lru_b_i": nrm(ks[12], (N_EVEN, D_LRU), 0.02),
        "lru_lambda": lru_lambda,
        "dn_conv_w": nrm(ks[9], (N_EVEN, DN_CONV_W, 3 * D_DN), DN_CONV_W ** -0.5),
        "dn_a_log": dn_a_log,
        "dn_dt_bias": dn_dt_bias,
        "dn_norm_w": 1.0 + nrm(ks[13], (N_EVEN, HEAD_DIM), 0.02),
        "even_w_out": nrm(ks[14], (N_EVEN, D_LRU + D_DN, D_MODEL), (D_LRU + D_DN) ** -0.5),
        "odd_norm_w": 1.0 + nrm(ks[15], (N_ODD, D_MODEL), 0.02),
        "odd_w_in": nrm(ks[16], (N_ODD, D_MODEL, D_IN1), D_MODEL ** -0.5),
        "odd_conv_w": nrm(ks[17], (N_ODD, SC_CONV_W, D_SC), SC_CONV_W ** -0.5),
        "odd_w_out": nrm(ks[18], (N_ODD, D_SC, D_MODEL), D_SC ** -0.5),
        "final_norm_w": 1.0 + nrm(ks[19], (D_MODEL,), 0.02),
    }


def reference(x, even_norm_w, even_w_in, lru_conv_w, lru_conv_b, lru_w_r, lru_b_r,
              lru_w_i, lru_b_i, lru_lambda, dn_conv_w, dn_a_log, dn_dt_bias, dn_norm_w,
              even_w_out, odd_norm_w, odd_w_in, odd_conv_w, odd_w_out, final_norm_w):
    for layer in range(DEPTH):
        j = layer // 2
        if layer % 2 == 0:
            x = lru_deltanet_layer(x, even_norm_w[j], even_w_in[j], lru_conv_w[j], lru_conv_b[j],
                                   lru_w_r[j], lru_b_r[j], lru_w_i[j], lru_b_i[j], lru_lambda[j],
                                   dn_conv_w[j], dn_a_log[j], dn_dt_bias[j], dn_norm_w[j],
                                   even_w_out[j])
        else:
            x = shortconv_layer(x, odd_norm_w[j], odd_w_in[j], odd_conv_w[j], odd_w_out[j])
    return rmsnorm(x, final_norm_w)
```

```python
import numpy as np
from contextlib import ExitStack
import concourse.bass as bass
import concourse.mybir as mybir
from concourse.bass_utils import run_bass_kernel_spmd

F32 = mybir.dt.float32
BF16 = mybir.dt.bfloat16
AF = mybir.ActivationFunctionType
ALU = mybir.AluOpType

D = 4096
NTOK = 1024
HALO = 4
NT = NTOK + HALO
KC = D // 128
EPS = 1e-6
TT = [(0, 344), (344, 342), (686, 342)]
OWT = 512
ENGS = ("pe", "act", "dve", "pool", "sp")


class Res:
    __slots__ = ("w", "rs", "name", "sem", "dcnt", "multi")

    def __init__(self, name="", multi=False):
        self.w = {}
        self.rs = {}
        self.name = name
        self.sem = None
        self.dcnt = 0
        self.multi = multi


def _merge(d, tok):
    k = id(tok[0])
    if k not in d or d[k][1] < tok[1]:
        d[k] = tok


class Sched:
    def __init__(self, nc, es):
        self.nc, self.es = nc, es
        self.prog = {e: [] for e in ENGS}
        self.psem, self.pcnt = {}, {}
        for e in ("pe", "act", "dve", "pool"):
            self.psem[e] = es.enter_context(nc.semaphore("prog_" + e))
            self.pcnt[e] = 0
        self.seen = {e: {} for e in ENGS}
        self.nsem = 0
        self.semkey = {}
        self.named = {}
        self.gen = 0
        self.pe_sems = {id(self.psem["pe"])}

    def _key(self, sem):
        k = id(sem)
        self.semkey[k] = sem
        return k

    def newsem(self, name):
        self.nsem += 1
        return self.es.enter_context(self.nc.semaphore("%s_%d" % (name, self.nsem)))

    def _deps(self, reads, writes):
        deps = []
        for r in reads:
            deps.extend(r.w.values())
        for r in writes:
            if not r.multi:
                deps.extend(r.w.values())
            deps.extend(r.rs.values())
        return deps

    def _waits(self, eng, deps):
        best = {}
        for (sem, val) in deps:
            k = self._key(sem)
            if eng == "pe" and id(sem) in self.pe_sems:
                continue
            if self.seen[eng].get(k, 0) >= val:
                continue
            if best.get(k, 0) < val:
                best[k] = val
        out = []
        for k, val in best.items():
            self.seen[eng][k] = val
            out.append((self.semkey[k], val))
        return out

    def _commit(self, tok, reads, writes):
        for r in reads:
            _merge(r.rs, tok)
        for r in writes:
            if r.multi:
                _merge(r.w, tok)
            else:
                r.w = {id(tok[0]): tok}
                r.rs = {}

    def op(self, eng, fn, reads=(), writes=()):
        waits = self._waits(eng, self._deps(reads, writes))
        self.pcnt[eng] += 1
        tok = (self.psem[eng], self.pcnt[eng])
        self.prog[eng].append((waits, fn, self.psem[eng], 1))
        self._commit(tok, reads, writes)
        return tok

    def dma(self, q, out, in_, reads, writes, semres):
        waits = self._waits(q, self._deps(reads, writes))
        rec = self.named.setdefault(semres.name, [None, 0])
        if rec[0] is None:
            rec[0] = self.newsem("d")
        rec[1] += 16
        tok = (rec[0], rec[1])
        self.prog[q].append((waits, lambda e: e.dma_start(out=out, in_=in_), rec[0], 16))
        self._commit(tok, reads, writes)
        return tok

    def new_generation(self):
        self.gen += 1
        for e in ("pe", "act", "dve", "pool"):
            self.psem[e] = self.es.enter_context(self.nc.semaphore("prog_%s_g%d" % (e, self.gen)))
            self.pcnt[e] = 0
            if e == "pe":
                self.pe_sems.add(id(self.psem[e]))

    def barrier(self):
        toks = [(self.psem[e], self.pcnt[e]) for e in self.psem if self.pcnt[e] > 0]
        toks += [(rec[0], rec[1]) for rec in self.named.values() if rec[1] > 0]
        for e in ENGS:
            waits = self._waits(e, toks)
            if waits:
                self.prog[e].append((waits, None, None, 0))

    def wait_all(self, eng, ress):
        deps = []
        for r in ress:
            deps.extend(r.w.values())
        waits = self._waits(eng, deps)
        self.prog[eng].append((waits, None, None, 0))

    def finalize(self):
        nc = self.nc
        block = self.es.enter_context(nc.Block())
        names = {"pe": "tensor", "act": "scalar", "dve": "vector", "pool": "gpsimd", "sp": "sync"}

        def mk(eng):
            def body(e):
                for (waits, fn, sem, n) in self.prog[eng]:
                    for (s, v) in waits:
                        e.wait_ge(s, v)
                    if fn is not None:
                        fn(e).then_inc(sem, n)
            return body

        for eng in ENGS:
            if self.prog[eng]:
                getattr(block, names[eng])(mk(eng))


class Ctx:
    pass


_UNIQ = [0]


def alloc(c, name, shape, dt):
    _UNIQ[0] += 1
    return c.es.enter_context(c.nc.sbuf_tensor("%s_u%d" % (name, _UNIQ[0]), shape, dt))


def setup_common(nc, es):
    c = Ctx()
    c.nc, c.es = nc, es
    c.S = Sched(nc, es)
    c.ps = [es.enter_context(nc.psum_tensor("psb%d" % i, [128, 512], F32)) for i in range(8)]
    c.psr = [Res("ps%d" % i) for i in range(8)]
    return c


def load_const(c, name, dram_ap, shape, dt=F32, q="pool"):
    t = alloc(c, name, shape, dt)
    r = Res(name)
    idx = tuple(slice(None) for _ in shape)
    c.S.dma(q, t[idx], dram_ap, [], [r], r)
    return t, r


def norm_to_hT(c, x_dram, nwfm, nwfm_r, ident, ident_r, hT, hT_r, scope, sel=None):
    S = c.S
    xt = [alloc(scope, "xt%d" % i, [128, D], F32) for i in range(2)]
    xt_r = [Res("xt%d" % i) for i in range(2)]
    xs = alloc(scope, "xs", [128, D], F32)
    xs_r = Res("xs")
    st = alloc(scope, "nstat", [128, 4], F32)
    st_r = Res("nstat")
    tiles = [(0, HALO)] + [(HALO + 128 * i, 128) for i in range(NTOK // 128)]
    for ti, (r0, n) in enumerate(tiles):
        b = ti % 2
        if sel is None:
            S.dma("sp", xt[b][0:n, :], x_dram[r0:r0 + n, :], [], [xt_r[b]], xt_r[b])
        else:
            xfull, msel, msel_r, xown, xq, xq_r, own_rs = sel
            for sq in range(4):
                qb = (ti * 4 + sq) % 2
                S.dma("sp", xq[qb][0:n, :], xfull[sq * NTOK + r0:sq * NTOK + r0 + n, :], [], [xq_r[qb]], xq_r[qb])
                if sq == 0:
                    S.op("dve", lambda e, b=b, n=n, qb=qb: e.tensor_scalar(out=xt[b][0:n, :], in0=xq[qb][0:n, :],
                                                                           scalar1=msel[0:n, 0:1], scalar2=None,
                                                                           op0=ALU.mult), [xq_r[qb], msel_r], [xt_r[b]])
                else:
                    S.op("dve", lambda e, b=b, n=n, qb=qb, sq=sq: e.scalar_tensor_tensor(
                        out=xt[b][0:n, :], in0=xq[qb][0:n, :], scalar=msel[0:n, sq:sq + 1], in1=xt[b][0:n, :],
                        op0=ALU.mult, op1=ALU.add), [xq_r[qb], msel_r], [xt_r[b]])
            orr = Res("xown")
            S.dma("sp", xown[r0:r0 + n, :], xt[b][0:n, :], [xt_r[b]], [orr], xt_r[b])
            own_rs.append(orr)
        S.op("act", lambda e, b=b, n=n: e.activation(out=xs[0:n, :], in_=xt[b][0:n, :], func=AF.Square,
                                                     accum_out=st[0:n, 0:1]),
             [xt_r[b]], [xs_r, st_r])
        S.op("dve", lambda e, n=n: e.tensor_scalar(out=st[0:n, 1:2], in0=st[0:n, 0:1], scalar1=1.0 / D, scalar2=EPS,
                                                   op0=ALU.mult, op1=ALU.add), [st_r], [st_r])
        S.op("act", lambda e, n=n: e.activation(out=st[0:n, 2:3], in_=st[0:n, 1:2], func=AF.Sqrt), [st_r], [st_r])
        S.op("dve", lambda e, n=n: e.reciprocal(out=st[0:n, 3:4], in_=st[0:n, 2:3]), [st_r], [st_r])
        S.op("dve", lambda e, b=b, n=n: e.tensor_scalar(out=xs[0:n, :], in0=xt[b][0:n, :], scalar1=st[0:n, 3:4],
                                                        scalar2=None, op0=ALU.mult), [xt_r[b], st_r], [xs_r])
        for kc in range(KC):
            pb = 4 + (kc // 4) % 4
            q = kc % 4
            S.op("pe", lambda e, kc=kc, pb=pb, q=q, n=n: e.transpose(out=c.ps[pb][:, q * 128:q * 128 + n],
                                                                      in_=xs[0:n, kc * 128:(kc + 1) * 128],
                                                                      identity=ident[0:n, 0:n]),
                 [xs_r, ident_r], [c.psr[pb]])
            if q == 3:
                for qq in range(4):
                    k2 = kc - 3 + qq
                    S.op("act", lambda e, k2=k2, pb=pb, qq=qq, n=n, r0=r0: e.activation(
                        out=hT[:, k2, r0:r0 + n], in_=c.ps[pb][:, qq * 128:qq * 128 + n], func=AF.Copy,
                        scale=nwfm[:, k2:k2 + 1]), [c.psr[pb], nwfm_r], [hT_r])


def proj_fm(c, wslot, wcol0, hT, hT_r, wr, bankset):
    S = c.S
    for kc in range(KC):
        for ti, (t0, tn) in enumerate(TT):
            pb = bankset[ti]
            S.op("pe", lambda e, kc=kc, pb=pb, t0=t0, tn=tn: e.matmul(
                c.ps[pb][:, 0:tn], wslot[:, kc, wcol0:wcol0 + 128], hT[:, kc, t0:t0 + tn],
                start=(kc == 0), stop=(kc == KC - 1)), [wr, hT_r], [c.psr[pb]])


def evac_fm(c, bankset, dst, dst_r, func=None, eng_rot=("act", "dve", "act")):
    S = c.S
    for ti, (t0, tn) in enumerate(TT):
        pb = bankset[ti]
        eng = eng_rot[ti] if func is None else "act"
        if eng == "act":
            f = AF.Copy if func is None else func
            S.op("act", lambda e, pb=pb, t0=t0, tn=tn, f=f: e.activation(out=dst[:, t0:t0 + tn], in_=c.ps[pb][:, 0:tn],
                                                                        func=f), [c.psr[pb]], [dst_r])
        else:
            S.op("dve", lambda e, pb=pb, t0=t0, tn=tn: e.tensor_copy(out=dst[:, t0:t0 + tn], in_=c.ps[pb][:, 0:tn]),
                 [c.psr[pb]], [dst_r])


def out_proj(c, ybig, ybig_r, w_dram, xres_dram, xres_row0, out_dram, scope, ssq=None, ssq_r=None):
    S = c.S
    WT = OWT
    nct = D // WT
    NWO = 2
    wsl = [alloc(scope, "wo%d" % i, [128, KC, WT], BF16) for i in range(NWO)]
    wsl_r = [Res("wo%d" % i) for i in range(NWO)]
    xr = [alloc(scope, "xr%d" % i, [128, WT], F32) for i in range(3)]
    xr_r = [Res("xr%d" % i) for i in range(3)]
    ot = [alloc(scope, "ot%d" % i, [128, WT], F32) for i in range(3)]
    ot_r = [Res("ot%d" % i) for i in range(3)]
    junk = alloc(scope, "ojunk", [128, WT], F32)
    junk_r = Res("ojunk")
    wv = w_dram.rearrange("(k p) n -> p k n", p=128)
    out_rs = []
    it = 0
    for ct in range(nct):
        ws = ct % NWO
        S.dma("pool", wsl[ws][:, :, :], wv[:, :, ct * WT:(ct + 1) * WT], [], [wsl_r[ws]], wsl_r[ws])
        for i in range(NTOK // 128):
            pb = it % 2
            sl = it % 3
            it += 1
            S.dma("sp", xr[sl][:, :], xres_dram[xres_row0 + i * 128:xres_row0 + (i + 1) * 128, ct * WT:(ct + 1) * WT],
                  [], [xr_r[sl]], xr_r[sl])
            for kc in range(KC):
                S.op("pe", lambda e, kc=kc, pb=pb, i=i, ws=ws: e.matmul(
                    c.ps[pb][:, 0:WT], ybig[:, kc, i * 128:(i + 1) * 128], wsl[ws][:, kc, :],
                    start=(kc == 0), stop=(kc == KC - 1)), [ybig_r, wsl_r[ws]], [c.psr[pb]])
            S.op("dve", lambda e, pb=pb, sl=sl: e.tensor_tensor(out=ot[sl][:, :], in0=c.ps[pb][:, 0:WT], in1=xr[sl][:, :],
                                                               op=ALU.add), [c.psr[pb], xr_r[sl]], [ot_r[sl]])
            if ssq is not None:
                S.op("act", lambda e, sl=sl, i=i, ct=ct: e.activation(out=junk[:, :], in_=ot[sl][:, :], func=AF.Square,
                                                                     accum_out=ssq[:, i * nct + ct:i * nct + ct + 1]),
                     [ot_r[sl]], [junk_r, ssq_r])
            orr = Res("o")
            S.dma("sp", out_dram[i * 128:(i + 1) * 128, ct * WT:(ct + 1) * WT], ot[sl][:, :], [ot_r[sl]], [orr], ot_r[sl])
            out_rs.append(orr)
    return out_rs


def phase_c_decl(nc, pfx="", with_x1e=True):
    di = lambda name, shape: nc.dram_tensor(pfx + name, shape, F32, kind="ExternalInput").ap()
    T = {}
    if with_x1e:
        T["x1e"] = di("x1e", [NT, D])
    T["nw_d"] = di("nwfm", [128, KC]); T["w_in"] = di("w_in", [D, 4 * D]); T["cw_d"] = di("cw", [128, KC * 3])
    T["w_out"] = di("w_out", [D, D]); T["fnw_d"] = di("fnw", [128, D]); T["ident_d"] = di("ident", [128, 128])
    T["out"] = nc.dram_tensor("out", [NTOK, D], F32, kind="ExternalOutput").ap()
    T["yscr"] = nc.dram_tensor(pfx + "yscr", [D, NTOK], BF16, kind="Internal").ap()
    return T


def build_phase_c(dbg=False):
    nc = bass.Bass("TRN2", target_bir_lowering=False)
    T = phase_c_decl(nc)
    es = ExitStack()
    with es:
        c = setup_common(nc, es)
        phase_c_body(c, es, nc, T)
        c.S.finalize()
    return nc


def phase_c_body(c0, es, nc, T, sel=None):
    dbg = False
    dbgo = None
    x1e, nw_d, w_in, cw_d, w_out, fnw_d, ident_d, out, yscr = (T.get("x1e"), T["nw_d"], T["w_in"], T["cw_d"], T["w_out"],
                                                               T["fnw_d"], T["ident_d"], T["out"], T["yscr"])
    if True:
        c = Ctx()
        c.nc, c.es, c.S, c.ps, c.psr = nc, es, c0.S, c0.ps, c0.psr
        S = c.S
        nwfm, nwfm_r = load_const(c, "nwfm_s", nw_d, [128, KC])
        cw, cw_r = load_const(c, "cw_s", cw_d, [128, KC * 3])
        ident, ident_r = load_const(c, "ident_s", ident_d, [128, 128])
        big = alloc(c, "big", [128, KC, NT], BF16)
        big_r = Res("big", multi=True)
        ssq = alloc(c, "ssq", [128, 8 * (D // OWT)], F32)
        ssq_r = Res("ssq", multi=True)
        with ExitStack() as s1:
            sc = Ctx(); sc.nc, sc.es = nc, s1
            if sel is not None:
                xfull, msel_d, xown = sel
                msel, msel_r = load_const(sc_as(c, sc), "msel_s", msel_d, [128, 4])
                xq = [alloc(sc, "xq%d" % i, [128, D], F32) for i in range(2)]
                xq_r = [Res("xq%d" % i) for i in range(2)]
                norm_to_hT(c, None, nwfm, nwfm_r, ident, ident_r, big, big_r, sc,
                           sel=(xfull, msel, msel_r, xown, xq, xq_r, []))
                x1e = xown
            else:
                norm_to_hT(c, x1e, nwfm, nwfm_r, ident, ident_r, big, big_r, sc)
            S.barrier()
        yscr_rs = []
        with ExitStack() as s2:
            sc = Ctx(); sc.nc, sc.es = nc, s2
            WT = 256
            NW = 4
            wsl = [alloc(sc, "wi%d" % i, [128, KC, WT], BF16) for i in range(NW)]
            wsl_r = [Res("wi%d" % i) for i in range(NW)]
            fm = {}
            fm_r = {}
            for hh in range(2):
                for nm in ("xin", "cc", "bb", "sz"):
                    fm[nm, hh] = alloc(sc, "fm_%s%d" % (nm, hh), [128, NT], F32)
                    fm_r[nm, hh] = Res("fm_%s%d" % (nm, hh), multi=True)
            cx = alloc(sc, "fm_cx", [128, NT], F32)
            cx_r = Res("fm_cx")
            cv = alloc(sc, "fm_cv", [128, NTOK], F32)
            cv_r = Res("fm_cv")
            yst = [alloc(sc, "yst%d" % i, [128, NTOK], BF16) for i in range(2)]
            yst_r = [Res("yst%d" % i) for i in range(2)]
            wv = w_in.rearrange("(k p) n -> p k n", p=128)
            wi = 0
            bs = 0
            for cp in range(D // WT):
                for g, nm, fn in ((0, "xin", None), (2, "cc", None), (1, "bb", None), (3, "sz", AF.Silu)):
                    sl = wi % NW
                    wi += 1
                    S.dma("pool", wsl[sl][:, :, :], wv[:, :, g * D + cp * WT:g * D + (cp + 1) * WT], [], [wsl_r[sl]],
                          wsl_r[sl])
                    for hh in range(2):
                        bankset = (0, 1, 2) if bs % 2 == 0 else (3, 4, 5)
                        bs += 1
                        proj_fm(c, wsl[sl], hh * 128, big, big_r, wsl_r[sl], bankset)
                        evac_fm(c, bankset, fm[nm, hh], fm_r[nm, hh], func=fn)
                for hh in range(2):
                    ch = cp * 2 + hh
                    S.op("dve", lambda e, hh=hh: e.tensor_tensor(out=cx[:, :], in0=fm["cc", hh][:, :],
                                                                 in1=fm["xin", hh][:, :], op=ALU.mult),
                         [fm_r["cc", hh], fm_r["xin", hh]], [cx_r])
                    S.op("dve", lambda e, ch=ch: e.tensor_scalar(out=cv[:, :], in0=cx[:, 2:2 + NTOK],
                                                                 scalar1=cw[:, ch * 3:ch * 3 + 1], scalar2=None,
                                                                 op0=ALU.mult), [cx_r, cw_r], [cv_r])
                    for k in (1, 2):
                        S.op("dve", lambda e, ch=ch, k=k: e.scalar_tensor_tensor(
                            out=cv[:, :], in0=cx[:, 2 + k:2 + k + NTOK],
                            scalar=cw[:, ch * 3 + k:ch * 3 + k + 1], in1=cv[:, :], op0=ALU.mult, op1=ALU.add),
                            [cx_r, cw_r], [cv_r])
                    S.op("dve", lambda e, hh=hh: e.tensor_tensor(out=cv[:, :], in0=cv[:, :],
                                                                 in1=fm["bb", hh][:, HALO:NT], op=ALU.mult),
                         [fm_r["bb", hh]], [cv_r])
                    yb = ch % 2
                    S.op("dve", lambda e, yb=yb, hh=hh: e.tensor_tensor(out=yst[yb][:, :], in0=cv[:, :],
                                                                        in1=fm["sz", hh][:, HALO:NT], op=ALU.mult),
                         [cv_r, fm_r["sz", hh]], [yst_r[yb]])
                    yr = Res("yscr")
                    S.dma("sp", yscr[ch * 128:(ch + 1) * 128, :], yst[yb][:, :], [yst_r[yb]], [yr], yst_r[yb])
                    yscr_rs.append(yr)
                    if dbg and ch == 0:
                        for di, nm in enumerate(("xin", "cc", "bb", "sz")):
                            yr = Res("dbg")
                            S.dma("pool", dbgo[:, di * NT:(di + 1) * NT], fm[nm, 0][:, :], [fm_r[nm, 0]], [yr], fm_r[nm, 0])
                            yscr_rs.append(yr)
                        yr = Res("dbg")
                        S.dma("pool", dbgo[:, 4 * NT:4 * NT + NTOK], cv[:, :], [cv_r], [yr], cv_r)
                        yscr_rs.append(yr)
                if dbg and cp == 0:
                    break
            S.barrier()
        with ExitStack() as s3:
            sc = Ctx(); sc.nc, sc.es = nc, s3
            nct = D // OWT
            if dbg:
                S.wait_all("pool", yscr_rs)
                S.wait_all("sp", yscr_rs)
                o_rs = []
            else:
                S.dma("sp", big[:, :, 0:NTOK], yscr.rearrange("(k p) t -> p k t", p=128), yscr_rs, [big_r], big_r)
                o_rs = out_proj(c, big, big_r, w_out, x1e, HALO, out, sc, ssq, ssq_r)
            S.barrier()
        with ExitStack() as s4:
            sc = Ctx(); sc.nc, sc.es = nc, s4
            fnw, fnw_r = load_const(sc_as(c, sc), "fnw_s", fnw_d, [128, D])
            xt = [alloc(sc, "fx%d" % i, [128, D], F32) for i in range(2)]
            xt_r = [Res("fx%d" % i) for i in range(2)]
            st = alloc(sc, "fst", [128, 4], F32)
            st_r = Res("fst")
            fin = []
            if dbg:
                S.wait_all("sp", o_rs)
            for i in range(0 if dbg else 8):
                b = i % 2
                S.dma("sp", xt[b][:, :], out[i * 128:(i + 1) * 128, :], o_rs, [xt_r[b]], xt_r[b])
                S.op("dve", lambda e, i=i: e.tensor_reduce(out=st[:, 0:1], in_=ssq[:, i * nct:(i + 1) * nct],
                                                           axis=mybir.AxisListType.X, op=ALU.add), [ssq_r], [st_r])
                S.op("dve", lambda e: e.tensor_scalar(out=st[:, 1:2], in0=st[:, 0:1], scalar1=1.0 / D, scalar2=EPS,
                                                      op0=ALU.mult, op1=ALU.add), [st_r], [st_r])
                S.op("act", lambda e: e.activation(out=st[:, 2:3], in_=st[:, 1:2], func=AF.Sqrt), [st_r], [st_r])
                S.op("dve", lambda e: e.reciprocal(out=st[:, 3:4], in_=st[:, 2:3]), [st_r], [st_r])
                S.op("dve", lambda e, b=b: e.scalar_tensor_tensor(out=xt[b][:, :], in0=xt[b][:, :], scalar=st[:, 3:4],
                                                                  in1=fnw[:, :], op0=ALU.mult, op1=ALU.mult),
                     [st_r, fnw_r], [xt_r[b]])
                fr = Res("fin")
                S.dma("sp", out[i * 128:(i + 1) * 128, :], xt[b][:, :], [xt_r[b]], [fr], xt_r[b])
                fin.append(fr)
            S.wait_all("sp", fin)


def sc_as(c, sc):
    n = Ctx()
    n.nc, n.es, n.S = c.nc, sc.es, c.S
    return n


def fm_vec(v):
    v = np.ascontiguousarray(v, dtype=np.float32)
    return np.ascontiguousarray(v.reshape(-1, 128).T)


def with_halo(xfull, b, s):
    o = np.zeros((NT, xfull.shape[-1]), np.float32)
    o[HALO:] = xfull[b, s * NTOK:(s + 1) * NTOK]
    if s > 0:
        o[:HALO] = xfull[b, s * NTOK - HALO:s * NTOK]
    return o


def run_phase_c(x1, odd_norm_w, odd_w_in, odd_conv_w, odd_w_out, final_norm_w, trace=False, dbg=False):
    nc = build_phase_c(dbg)
    cwl = np.ascontiguousarray(np.stack([fm_vec(odd_conv_w[0, k]) for k in range(3)], axis=-1).reshape(128, KC * 3))
    common = {
        "nwfm": fm_vec(odd_norm_w[0]), "w_in": np.ascontiguousarray(odd_w_in[0]), "cw": cwl,
        "w_out": np.ascontiguousarray(odd_w_out[0]),
        "fnw": np.ascontiguousarray(np.broadcast_to(final_norm_w.astype(np.float32), (128, D))),
        "ident": np.eye(128, dtype=np.float32),
    }
    in_maps = []
    for core in range(8):
        b, s = divmod(core, 4)
        m = dict(common)
        m["x1e"] = with_halo(x1, b, s)
        in_maps.append(m)
    res = run_bass_kernel_spmd(nc, in_maps, core_ids=list(range(8)), trace=trace)
    out = np.zeros((2, 4 * NTOK, D), np.float32)
    for core in range(8):
        b, s = divmod(core, 4)
        out[b, s * NTOK:(s + 1) * NTOK] = res.results[core]["out"]
    return out, res


NSEG = 4
DL = 2048
NH = 16
DIN0 = 12320


def phase_a_decl(nc, pfx=""):
    di = lambda name, shape: nc.dram_tensor(pfx + name, shape, F32, kind="ExternalInput").ap()
    T = {}
    T["xe"] = di("xe", [NSEG * NTOK + HALO, D])
    T["nw_d"] = di("nwfm", [128, KC])
    T["w_in"] = di("w_in", [D, DIN0])
    T["w_out"] = di("w_out", [D, D])
    T["lcw_d"] = di("lcw", [128, NH * 4]); T["lcb_d"] = di("lcb", [128, NH])
    T["lbr_d"] = di("lbr", [128, NH]); T["lbi_d"] = di("lbi", [128, NH]); T["lam_d"] = di("lam", [128, NH])
    T["wr_d"] = di("wr", [NH, 128, 128]); T["wi_d"] = di("wi", [NH, 128, 128])
    T["dcw_d"] = di("dcw", [128, 48 * 4])
    T["alog_d"] = di("alog_b", [128, NH]); T["dtb_d"] = di("dtb_b", [128, NH]); T["dnw_d"] = di("dnw", [128, 1])
    T["ident_d"] = di("ident", [128, 128]); T["ones_d"] = di("ones", [128, 128]); T["ut_d"] = di("ut", [128, 128])
    T["mls_d"] = di("mls", [128, 128]); T["mus_d"] = di("mus", [128, 128]); T["mui_d"] = di("mui", [128, 128])
    T["yscr"] = nc.dram_tensor(pfx + "yscr", [D, NTOK], BF16, kind="Internal").ap()
    return T


def build_phase_a(nseg=NSEG, dbg=False, stop=None):
    nc = bass.Bass("TRN2", target_bir_lowering=False)
    T = phase_a_decl(nc)
    T["x1"] = nc.dram_tensor("x1", [NSEG * NTOK, D], F32, kind="ExternalOutput").ap()
    es = ExitStack()
    with es:
        c = setup_common(nc, es)
        phase_a_body(c, es, nc, T, nseg)
        c.S.finalize()
    return nc


def phase_a_body(c0, es, nc, T, nseg=NSEG, stop=None):
    (xe, nw_d, w_in, w_out, lcw_d, lcb_d, lbr_d, lbi_d, lam_d, wr_d, wi_d, dcw_d, alog_d, dtb_d, dnw_d, ident_d, ones_d,
     ut_d, mls_d, mus_d, mui_d, yscr, x1) = [T[k] for k in (
        "xe", "nw_d", "w_in", "w_out", "lcw_d", "lcb_d", "lbr_d", "lbi_d", "lam_d", "wr_d", "wi_d", "dcw_d", "alog_d",
        "dtb_d", "dnw_d", "ident_d", "ones_d", "ut_d", "mls_d", "mus_d", "mui_d", "yscr", "x1")]
    if True:
        c = Ctx()
        c.nc, c.es, c.S, c.ps, c.psr = nc, es, c0.S, c0.ps, c0.psr
        S = c.S
        nwfm, nwfm_r = load_const(c, "nwfm_s", nw_d, [128, KC])
        ident, ident_r = load_const(c, "ident_s", ident_d, [128, 128])
        ones, ones_r = load_const(c, "ones_s", ones_d, [128, 128])
        ut, ut_r = load_const(c, "ut_s", ut_d, [128, 128])
        mls, mls_r = load_const(c, "mls_s", mls_d, [128, 128])
        mus, mus_r = load_const(c, "mus_s", mus_d, [128, 128])
        mui, mui_r = load_const(c, "mui_s", mui_d, [128, 128])
        lcw, lcw_r = load_const(c, "lcw_s", lcw_d, [128, NH * 4])
        lcb, lcb_r = load_const(c, "lcb_s", lcb_d, [128, NH])
        lbr, lbr_r = load_const(c, "lbr_s", lbr_d, [128, NH])
        lbi, lbi_r = load_const(c, "lbi_s", lbi_d, [128, NH])
        lam, lam_r = load_const(c, "lam_s", lam_d, [128, NH])
        dcw, dcw_r = load_const(c, "dcw_s", dcw_d, [128, 48 * 4])
        alog, alog_r = load_const(c, "alog_s", alog_d, [128, NH])
        dtb, dtb_r = load_const(c, "dtb_s", dtb_d, [128, NH])
        dnw, dnw_r = load_const(c, "dnw_s", dnw_d, [128, 1])
        hst = alloc(c, "hst", [128, NH], F32); hst_r = Res("hst")
        Sst = alloc(c, "Sst", [128, NH, 128], F32); Sst_r = [Res("S%d" % j) for j in range(NH)]
        cc1 = alloc(c, "cc1", [128, NH], F32); cc1_r = Res("cc1")
        cc2 = alloc(c, "cc2", [128, NH], F32); cc2_r = Res("cc2")
        negA = alloc(c, "negA", [128, NH], F32); negA_r = Res("negA")
        S.op("dve", lambda e: e.memset(hst[:, :], 0.0), [], [hst_r])
        S.op("dve", lambda e: e.memset(Sst[:, :, :], 0.0), [], Sst_r)
        S.op("act", lambda e: e.activation(out=cc1[:, :], in_=lam[:, :], func=AF.Exp, scale=-1.0), [lam_r], [cc1_r])
        S.op("act", lambda e: e.activation(out=cc1[:, :], in_=cc1[:, :], func=AF.Ln, bias=1.0), [], [cc1_r])
        S.op("dve", lambda e: e.tensor_scalar(out=cc2[:, :], in0=cc1[:, :], scalar1=-16.0, scalar2=None, op0=ALU.mult),
             [cc1_r], [cc2_r])
        S.op("dve", lambda e: e.tensor_scalar(out=cc1[:, :], in0=cc1[:, :], scalar1=-8.0, scalar2=None, op0=ALU.mult),
             [cc2_r], [cc1_r])
        S.op("act", lambda e: e.activation(out=negA[:, :], in_=alog[:, :], func=AF.Exp), [alog_r], [negA_r])
        S.op("dve", lambda e: e.tensor_scalar(out=negA[:, :], in0=negA[:, :], scalar1=-1.0, scalar2=None, op0=ALU.mult),
             [], [negA_r])
        big = alloc(c, "big", [128, KC, NT], BF16)
        big_r = Res("big", multi=True)
        wv = w_in.rearrange("(k p) n -> p k n", p=128)
        bsn = [0]

        def nextbanks():
            bsn[0] += 1
            return (0, 1, 2) if bsn[0] % 2 == 0 else (3, 4, 5)

        for seg in range(nseg):
            if seg > 0:
                S.new_generation()
            with ExitStack() as s1:
                sc = Ctx(); sc.nc, sc.es = nc, s1
                norm_to_hT(c, xe[seg * NTOK:seg * NTOK + NT, :], nwfm, nwfm_r, ident, ident_r, big, big_r, sc)
                S.barrier()
            yscr_rs = []
            with ExitStack() as s2:
                sc = Ctx(); sc.nc, sc.es = nc, s2
                WT = 256
                wrp = [alloc(sc, "wr_s%d" % i, [128, 2, 128], F32) for i in range(2)]
                wrp_r = [Res("wr%d" % i) for i in range(2)]
                wip = [alloc(sc, "wi_s%d" % i, [128, 2, 128], F32) for i in range(2)]
                wip_r = [Res("wi%d" % i) for i in range(2)]
                wsl = [alloc(sc, "wl%d" % i, [128, KC, WT], BF16) for i in range(3)]
                wsl_r = [Res("wl%d" % i) for i in range(3)]
                raw = {}; raw_r = {}
                for par in range(2):
                    for g in range(2):
                        for hh in range(2):
                            raw[g, hh, par] = alloc(sc, "lraw%d%d%d" % (g, hh, par), [128, NT], F32)
                            raw_r[g, hh, par] = Res("lraw", multi=True)
                T = {}; T_r = {}
                for nm in ("xc", "r", "gi", "a", "t", "h"):
                    T[nm] = alloc(sc, "l_" + nm, [128, NTOK], F32)
                    T_r[nm] = Res("l_" + nm, multi=(nm in ("r", "gi")))
                yst = [alloc(sc, "lyst%d" % i, [128, NTOK], BF16) for i in range(2)]
                yst_r = [Res("lyst%d" % i) for i in range(2)]
                wn = [0]

                def lru_proj(cp):
                    wr, wr_r, wi, wi_r = wrp[cp % 2], wrp_r[cp % 2], wip[cp % 2], wip_r[cp % 2]
                    S.dma("pool", wr[:, :, :], wr_d[cp * 2:cp * 2 + 2].rearrange("h i j -> i h j"), [], [wr_r], wr_r)
                    S.dma("pool", wi[:, :, :], wi_d[cp * 2:cp * 2 + 2].rearrange("h i j -> i h j"), [], [wi_r], wi_r)
                    for g in range(2):
                        sl = wn[0] % 3
                        wn[0] += 1
                        S.dma("pool", wsl[sl][:, :, :], wv[:, :, g * DL + cp * WT:g * DL + (cp + 1) * WT], [],
                              [wsl_r[sl]], wsl_r[sl])
                        for hh in range(2):
                            bk = nextbanks()
                            proj_fm(c, wsl[sl], hh * 128, big, big_r, wsl_r[sl], bk)
                            evac_fm(c, bk, raw[g, hh, cp % 2], raw_r[g, hh, cp % 2])

                def lru_mix(cp):
                    wr, wr_r, wi, wi_r = wrp[cp % 2], wrp_r[cp % 2], wip[cp % 2], wip_r[cp % 2]
                    for hh in range(2):
                        j = cp * 2 + hh
                        xa, xa_r = raw[0, hh, cp % 2], raw_r[0, hh, cp % 2]
                        ga, ga_r = raw[1, hh, cp % 2], raw_r[1, hh, cp % 2]
                        S.op("dve", lambda e, j=j, xa=xa: e.tensor_scalar(
                            out=T["xc"][:, :], in0=xa[:, 1:1 + NTOK], scalar1=lcw[:, j * 4:j * 4 + 1],
                            scalar2=lcb[:, j:j + 1], op0=ALU.mult, op1=ALU.add), [xa_r, lcw_r, lcb_r], [T_r["xc"]])
                        for k in (1, 2, 3):
                            S.op("dve", lambda e, j=j, k=k, xa=xa: e.scalar_tensor_tensor(
                                out=T["xc"][:, :], in0=xa[:, 1 + k:1 + k + NTOK], scalar=lcw[:, j * 4 + k:j * 4 + k + 1],
                                in1=T["xc"][:, :], op0=ALU.mult, op1=ALU.add), [xa_r, lcw_r], [T_r["xc"]])
                        for half in range(2):
                            for (wt_, wt_r_, bias_, nm) in ((wr, wr_r, lbr, "r"), (wi, wi_r, lbi, "gi")):
                                pb = 6 + (half + (nm == "gi")) % 2
                                S.op("pe", lambda e, hh=hh, half=half, wt_=wt_, pb=pb: e.matmul(
                                    c.ps[pb][:, :], wt_[:, hh, :], T["xc"][:, half * 512:(half + 1) * 512],
                                    start=True, stop=True), [wt_r_, T_r["xc"]], [c.psr[pb]])
                                S.op("act", lambda e, j=j, half=half, bias_=bias_, nm=nm, pb=pb: e.activation(
                                    out=T[nm][:, half * 512:(half + 1) * 512], in_=c.ps[pb][:, :], func=AF.Sigmoid,
                                    bias=bias_[:, j:j + 1]), [c.psr[pb], lbr_r, lbi_r], [T_r[nm]])
                        S.op("act", lambda e, j=j: e.activation(out=T["a"][:, :], in_=T["r"][:, :], func=AF.Exp,
                                                                scale=cc1[:, j:j + 1]), [T_r["r"], cc1_r], [T_r["a"]])
                        S.op("act", lambda e, j=j: e.activation(out=T["t"][:, :], in_=T["r"][:, :], func=AF.Exp,
                                                                scale=cc2[:, j:j + 1]), [T_r["r"], cc2_r], [T_r["t"]])
                        S.op("dve", lambda e: e.tensor_scalar(out=T["t"][:, :], in0=T["t"][:, :], scalar1=-1.0, scalar2=1.0,
                                                              op0=ALU.mult, op1=ALU.add), [], [T_r["t"]])
                        S.op("act", lambda e: e.activation(out=T["t"][:, :], in_=T["t"][:, :], func=AF.Sqrt), [], [T_r["t"]])
                        S.op("dve", lambda e: e.tensor_tensor(out=T["gi"][:, :], in0=T["gi"][:, :], in1=T["xc"][:, :],
                                                              op=ALU.mult), [T_r["xc"], T_r["gi"]], [T_r["gi"]])
                        S.op("dve", lambda e: e.tensor_tensor(out=T["t"][:, :], in0=T["t"][:, :], in1=T["gi"][:, :],
                                                              op=ALU.mult), [T_r["gi"]], [T_r["t"]])
                        S.op("dve", lambda e, j=j: e.tensor_tensor_scan(out=T["h"][:, :], data0=T["a"][:, :],
                                                                        data1=T["t"][:, :], initial=hst[:, j:j + 1],
                                                                        op0=ALU.mult, op1=ALU.add),
                             [T_r["a"], T_r["t"], hst_r], [T_r["h"]])
                        S.op("dve", lambda e, j=j: e.tensor_copy(out=hst[:, j:j + 1], in_=T["h"][:, NTOK - 1:NTOK]),
                             [T_r["h"]], [hst_r])
                        S.op("act", lambda e, ga=ga: e.activation(out=T["xc"][:, :], in_=ga[:, HALO:NT], func=AF.Silu),
                             [ga_r], [T_r["xc"]])
                        yb = j % 2
                        S.op("dve", lambda e, yb=yb: e.tensor_tensor(out=yst[yb][:, :], in0=T["h"][:, :], in1=T["xc"][:, :],
                                                                     op=ALU.mult), [T_r["h"], T_r["xc"]], [yst_r[yb]])
                        yr = Res("yscr")
                        S.dma("sp", yscr[j * 128:(j + 1) * 128, :], yst[yb][:, :], [yst_r[yb]], [yr], yst_r[yb])
                        yscr_rs.append(yr)

                lru_proj(0)
                for cp in range(NH // 2):
                    if cp + 1 < NH // 2:
                        lru_proj(cp + 1)
                    lru_mix(cp)
                S.barrier()
            if stop == "lru":
                S.wait_all("sp", yscr_rs)
                break
            with ExitStack() as s3:
                sc = Ctx(); sc.nc, sc.es = nc, s3
                dn_segment(c, sc, big, big_r, wv, yscr, yscr_rs, Sst, Sst_r, dcw, dcw_r, negA, negA_r, dtb, dtb_r,
                           dnw, dnw_r, ident, ident_r, ones, ones_r, ut, ut_r, mls, mls_r, mus, mus_r, mui, mui_r,
                           nextbanks, stop)
                S.barrier()
            if stop in ("dnpro", "dn"):
                S.wait_all("sp", yscr_rs)
                break
            with ExitStack() as s4:
                sc = Ctx(); sc.nc, sc.es = nc, s4
                S.dma("sp", big[:, :, 0:NTOK], yscr.rearrange("(k p) t -> p k t", p=128), yscr_rs, [big_r], big_r)
                o_rs = out_proj(c, big, big_r, w_out, xe, HALO + seg * NTOK, x1[seg * NTOK:(seg + 1) * NTOK, :], sc)
                S.wait_all("sp", o_rs)
                S.barrier()


def dn_segment(c, sc, big, big_r, wv, yscr, yscr_rs, Sst, Sst_r, dcw, dcw_r, negA, negA_r, dtb, dtb_r, dnw, dnw_r,
               ident, ident_r, ones, ones_r, ut, ut_r, mls, mls_r, mus, mus_r, mui, mui_r, nextbanks, stop=None):
    S = c.S
    NCH = NTOK // 128
    wba = alloc(sc, "wba", [128, KC, 32], BF16); wba_r = Res("wba")
    S.dma("pool", wba[:, :, :], wv[:, :, 6 * DL:6 * DL + 32], [], [wba_r], wba_r)
    tm = {}
    tm_r = {}
    for nm in ("sqb", "g", "gc", "ngc", "nbe", "ekd", "egl", "tmp"):
        tm[nm] = alloc(sc, "tm_" + nm, [128, NCH, NH], F32)
        tm_r[nm] = Res("tm_" + nm)
    for i in range(NCH):
        for kc in range(KC):
            S.op("pe", lambda e, kc=kc, i=i: e.matmul(c.ps[6][:, 0:32], big[:, kc, HALO + i * 128:HALO + (i + 1) * 128],
                                                      wba[:, kc, :], start=(kc == 0), stop=(kc == KC - 1)),
                 [big_r, wba_r], [c.psr[6]])
        S.op("act", lambda e, i=i: e.activation(out=tm["sqb"][:, i, :], in_=c.ps[6][:, 0:16], func=AF.Sigmoid),
             [c.psr[6]], [tm_r["sqb"]])
        S.op("act", lambda e, i=i: e.activation(out=tm["sqb"][:, i, :], in_=tm["sqb"][:, i, :], func=AF.Sqrt),
             [], [tm_r["sqb"]])
        S.op("dve", lambda e, i=i: e.tensor_tensor(out=tm["g"][:, i, :], in0=c.ps[6][:, 16:32], in1=dtb[:, :], op=ALU.add),
             [c.psr[6], dtb_r], [tm_r["g"]])
        S.op("act", lambda e, i=i: e.activation(out=tm["g"][:, i, :], in_=tm["g"][:, i, :], func=AF.Exp), [], [tm_r["g"]])
        S.op("act", lambda e, i=i: e.activation(out=tm["g"][:, i, :], in_=tm["g"][:, i, :], func=AF.Ln, bias=1.0),
             [], [tm_r["g"]])
        S.op("dve", lambda e, i=i: e.tensor_tensor(out=tm["g"][:, i, :], in0=tm["g"][:, i, :], in1=negA[:, :], op=ALU.mult),
             [negA_r], [tm_r["g"]])
        S.op("pe", lambda e, i=i: e.matmul(c.ps[7][:, 0:16], ut[:, :], tm["g"][:, i, :], start=True, stop=True),
             [ut_r, tm_r["g"]], [c.psr[7]])
        S.op("pe", lambda e, i=i: e.matmul(c.ps[7][:, 16:32], ones[:, :], tm["g"][:, i, :], start=True, stop=True),
             [ones_r, tm_r["g"]], [c.psr[7]])
        S.op("dve", lambda e, i=i: e.tensor_copy(out=tm["gc"][:, i, :], in_=c.ps[7][:, 0:16]), [c.psr[7]], [tm_r["gc"]])
        S.op("dve", lambda e, i=i: e.tensor_scalar(out=tm["ngc"][:, i, :], in0=c.ps[7][:, 0:16], scalar1=-1.0, scalar2=None,
                                                   op0=ALU.mult), [c.psr[7]], [tm_r["ngc"]])
        S.op("act", lambda e, i=i: e.activation(out=tm["egl"][:, i, :], in_=c.ps[7][:, 16:32], func=AF.Exp),
             [c.psr[7]], [tm_r["egl"]])
        S.op("dve", lambda e, i=i: e.tensor_tensor(out=tm["tmp"][:, i, :], in0=c.ps[7][:, 16:32], in1=tm["gc"][:, i, :],
                                                   op=ALU.subtract), [c.psr[7], tm_r["gc"]], [tm_r["tmp"]])
        S.op("act", lambda e, i=i: e.activation(out=tm["ekd"][:, i, :], in_=tm["tmp"][:, i, :], func=AF.Exp),
             [tm_r["tmp"]], [tm_r["ekd"]])
        S.op("act", lambda e, i=i: e.activation(out=tm["tmp"][:, i, :], in_=tm["gc"][:, i, :], func=AF.Exp),
             [tm_r["gc"], tm_r["ekd"]], [tm_r["tmp"]])
        S.op("dve", lambda e, i=i: e.scalar_tensor_tensor(out=tm["nbe"][:, i, :], in0=tm["tmp"][:, i, :], scalar=-1.0,
                                                          in1=tm["sqb"][:, i, :], op0=ALU.mult, op1=ALU.mult),
             [tm_r["tmp"], tm_r["sqb"]], [tm_r["nbe"]])
    S.barrier()
    if stop == "dnpro":
        return
    WT = 128
    NWS = 4
    wsl = [alloc(sc, "wd%d" % i, [128, KC, WT], BF16) for i in range(NWS)]
    wsl_r = [Res("wd%d" % i) for i in range(NWS)]
    raw = {}; raw_r = {}
    for nm in ("q", "k", "v", "z"):
        raw[nm] = alloc(sc, "draw_" + nm, [128, NT], F32)
        raw_r[nm] = Res("draw_" + nm, multi=True)
    F = {}; F_r = {}
    for nm in ("qf", "kf", "vf", "knT", "t1", "epsq", "oT"):
        F[nm] = alloc(sc, "df_" + nm, [128, NTOK], F32)
        F_r[nm] = Res("df_" + nm, multi=(nm in ("t1", "epsq", "oT")))
    names = ["rl", "rhsb", "egc", "decs", "dTs", "dTi", "ktT", "qdT", "A0", "B0", "A1", "B1", "P", "attnT", "kd", "vs", "Rp", "vn"]
    outs_ = ("P", "attnT", "kd", "vs", "qdT")
    pool_t = {}
    for nm in names:
        depth = 4 if nm in outs_ else 2
        for k in range(depth):
            pool_t[nm, k] = (alloc(sc, "dc%d_%s" % (k, nm), [128, 256 if nm == "rhsb" else 128], F32),
                             Res("dc%d_%s" % (k, nm)))
    Cts = []
    for bset in range(4):
        Ct = {}; Ct_r = {}
        for nm in names:
            k = bset if nm in outs_ else bset % 2
            Ct[nm], Ct_r[nm] = pool_t[nm, k]
        Cts.append((Ct, Ct_r))
    yst = [alloc(sc, "dyst%d" % i, [128, NTOK], BF16) for i in range(2)]
    yst_r = [Res("dyst%d" % i) for i in range(2)]
    banks = [2, 3, 4, 7]
    qs_r = {b: c.psr[b] for b in range(8)}
    rot = [0]

    def pslot():
        rot[0] = (rot[0] + 1) % len(banks)
        return banks[rot[0]]

    def psap(b, n=128):
        return c.ps[b][:, 0:n]

    cnt = [0]

    def evac(dst, dst_r, src_bq, extra_reads=(), scale=None):
        cnt[0] += 1
        if scale is not None:
            S.op("act", lambda e: e.activation(out=dst, in_=psap(src_bq), func=AF.Copy, scale=scale),
                 [qs_r[src_bq]] + list(extra_reads), [dst_r])
        elif cnt[0] % 2 == 0:
            S.op("act", lambda e: e.activation(out=dst, in_=psap(src_bq), func=AF.Copy), [qs_r[src_bq]], [dst_r])
        else:
            S.op("dve", lambda e: e.tensor_copy(out=dst, in_=psap(src_bq)), [qs_r[src_bq]], [dst_r])

    def mm(bq, lhsT, rhs, reads, start=True, stop=True, n=128):
        S.op("pe", lambda e: e.matmul(psap(bq, n), lhsT, rhs, start=start, stop=stop), reads, [qs_r[bq]])

    wn = 0
    for j in range(NH):
        cols = {"q": 2 * DL + j * 128, "k": 3 * DL + j * 128, "v": 4 * DL + j * 128, "z": 5 * DL + j * 128}
        for nm in ("q", "k", "v", "z"):
            sl = wn % NWS
            wn += 1
            S.dma("pool", wsl[sl][:, :, :], wv[:, :, cols[nm]:cols[nm] + WT], [], [wsl_r[sl]], wsl_r[sl])
            bk = nextbanks()
            proj_fm(c, wsl[sl], 0, big, big_r, wsl_r[sl], bk)
            evac_fm(c, bk, raw[nm], raw_r[nm])
        for kind, (nm, dst) in enumerate((("q", "qf"), ("k", "kf"), ("v", "vf"))):
            ch = kind * NH + j
            S.op("dve", lambda e, nm=nm, ch=ch: e.tensor_scalar(out=F["t1"][:, :], in0=raw[nm][:, 1:1 + NTOK],
                                                                scalar1=dcw[:, ch * 4:ch * 4 + 1], scalar2=None,
                                                                op0=ALU.mult), [raw_r[nm], dcw_r], [F_r["t1"]])
            for k in (1, 2, 3):
                S.op("dve", lambda e, nm=nm, ch=ch, k=k: e.scalar_tensor_tensor(
                    out=F["t1"][:, :], in0=raw[nm][:, 1 + k:1 + k + NTOK], scalar=dcw[:, ch * 4 + k:ch * 4 + k + 1],
                    in1=F["t1"][:, :], op0=ALU.mult, op1=ALU.add), [raw_r[nm], dcw_r, F_r["t1"]], [F_r["t1"]])
            S.op("act", lambda e, dst=dst: e.activation(out=F[dst][:, :], in_=F["t1"][:, :], func=AF.Silu),
                 [F_r["t1"]], [F_r[dst]])
        for (src, which) in (("kf", "k"), ("qf", "q")):
            S.op("act", lambda e, src=src: e.activation(out=F["t1"][:, :], in_=F[src][:, :], func=AF.Square),
                 [F_r[src], F_r["t1"]], [F_r["t1"]])
            for half in range(2):
                S.op("pe", lambda e, half=half: e.matmul(c.ps[half][:, :], ones[:, :], F["t1"][:, half * 512:(half + 1) * 512],
                                                         start=True, stop=True), [ones_r, F_r["t1"]], [c.psr[half]])
            if which == "k":
                for half in range(2):
                    S.op("dve", lambda e, half=half: e.tensor_scalar(out=F["knT"][:, half * 512:(half + 1) * 512],
                                                                     in0=c.ps[half][:, :], scalar1=EPS, scalar2=None,
                                                                     op0=ALU.add), [c.psr[half]], [F_r["knT"]])
                S.op("act", lambda e: e.activation(out=F["knT"][:, :], in_=F["knT"][:, :], func=AF.Sqrt), [], [F_r["knT"]])
                S.op("dve", lambda e: e.reciprocal(out=F["knT"][:, :], in_=F["knT"][:, :]), [], [F_r["knT"]])
                S.op("dve", lambda e: e.tensor_tensor(out=F["knT"][:, :], in0=F["knT"][:, :], in1=F["kf"][:, :], op=ALU.mult),
                     [F_r["kf"]], [F_r["knT"]])
            else:
                for half in range(2):
                    S.op("dve", lambda e, half=half: e.tensor_scalar(out=F["epsq"][:, half * 512:(half + 1) * 512],
                                                                     in0=c.ps[half][:, :], scalar1=EPS, scalar2=128.0 * EPS,
                                                                     op0=ALU.add, op1=ALU.mult),
                         [c.psr[half], F_r["epsq"]], [F_r["epsq"]])
        Sj = Sst[:, j, :]
        Sj_r = Sst_r[j]

        def prep(n, bset):
            Ct, Ct_r = Cts[bset]
            csl = slice(n * 128, (n + 1) * 128)
            pbk = 6 if n % 2 == 0 else 5
            S.op("dve", lambda e, n=n, j=j: e.tensor_scalar(out=Ct["rhsb"][:, 0:128], in0=ident[:, :],
                                                            scalar1=tm["gc"][:, n, j:j + 1], scalar2=None, op0=ALU.mult),
                 [ident_r, tm_r["gc"]], [Ct_r["rhsb"]])
            yield
            S.op("dve", lambda e, n=n, j=j: e.tensor_scalar(out=Ct["rhsb"][:, 128:256], in0=ident[:, :],
                                                            scalar1=tm["sqb"][:, n, j:j + 1], scalar2=None, op0=ALU.mult),
                 [ident_r, tm_r["sqb"]], [Ct_r["rhsb"]])
            yield
            S.op("pe", lambda e: e.matmul(c.ps[pbk][:, 0:256], ones[:, :], Ct["rhsb"][:, :], start=True, stop=True),
                 [ones_r, Ct_r["rhsb"]], [qs_r[pbk]])
            yield
            S.op("act", lambda e: e.activation(out=Ct["egc"][:, :], in_=c.ps[pbk][:, 0:128], func=AF.Exp),
                 [qs_r[pbk]], [Ct_r["egc"]])
            yield
            S.op("act", lambda e, n=n, j=j: e.activation(out=Ct["rl"][:, :], in_=c.ps[pbk][:, 0:128], func=AF.Relu,
                                                         scale=1.0, bias=tm["ngc"][:, n, j:j + 1]),
                 [qs_r[pbk], tm_r["ngc"]], [Ct_r["rl"]])
            yield
            S.op("act", lambda e: e.activation(out=Ct["decs"][:, :], in_=Ct["rl"][:, :], func=AF.Exp, scale=-1.0),
                 [Ct_r["rl"]], [Ct_r["decs"]])
            yield
            S.op("act", lambda e, n=n, j=j: e.activation(out=Ct["rl"][:, :], in_=c.ps[pbk][:, 0:128], func=AF.Relu,
                                                         scale=-1.0, bias=tm["gc"][:, n, j:j + 1]),
                 [qs_r[pbk], tm_r["gc"]], [Ct_r["rl"]])
            yield
            S.op("act", lambda e: e.activation(out=Ct["dTi"][:, :], in_=Ct["rl"][:, :], func=AF.Exp, scale=-1.0),
                 [Ct_r["rl"]], [Ct_r["dTi"]])
            yield
            S.op("dve", lambda e: e.tensor_tensor(out=Ct["decs"][:, :], in0=Ct["decs"][:, :], in1=mls[:, :], op=ALU.mult),
                 [mls_r], [Ct_r["decs"]])
            yield
            S.op("dve", lambda e: e.tensor_tensor(out=Ct["dTs"][:, :], in0=Ct["dTi"][:, :], in1=mus[:, :], op=ALU.mult),
                 [mus_r, Ct_r["dTi"]], [Ct_r["dTs"]])
            yield
            S.op("dve", lambda e: e.tensor_tensor(out=Ct["dTi"][:, :], in0=Ct["dTi"][:, :], in1=mui[:, :], op=ALU.mult),
                 [mui_r, Ct_r["dTs"]], [Ct_r["dTi"]])
            yield
            S.op("dve", lambda e, csl=csl: e.tensor_tensor(out=Ct["ktT"][:, :], in0=F["knT"][:, csl], in1=c.ps[pbk][:, 128:256],
                                                           op=ALU.mult), [F_r["knT"], qs_r[pbk]], [Ct_r["ktT"]])
            yield
            S.op("dve", lambda e, csl=csl: e.tensor_tensor(out=Ct["qdT"][:, :], in0=F["qf"][:, csl], in1=Ct["egc"][:, :],
                                                           op=ALU.mult), [F_r["qf"], Ct_r["egc"]], [Ct_r["qdT"]])
            yield
            g_bq = pslot()
            mm(g_bq, Ct["ktT"][:, :], Ct["ktT"][:, :], [Ct_r["ktT"]])
            yield
            S.op("dve", lambda e, g_bq=g_bq: e.tensor_tensor(out=Ct["A0"][:, :], in0=psap(g_bq), in1=Ct["decs"][:, :],
                                                             op=ALU.mult), [qs_r[g_bq], Ct_r["decs"]], [Ct_r["A0"]])
            yield
            S.op("dve", lambda e, g_bq=g_bq: e.tensor_tensor(out=Ct["B0"][:, :], in0=psap(g_bq), in1=Ct["dTs"][:, :],
                                                             op=ALU.mult), [qs_r[g_bq], Ct_r["dTs"]], [Ct_r["B0"]])
            yield
            a_bq = pslot()
            mm(a_bq, F["knT"][:, csl], F["qf"][:, csl], [F_r["knT"], F_r["qf"]])
            yield
            S.op("dve", lambda e, a_bq=a_bq: e.tensor_tensor(out=Ct["attnT"][:, :], in0=psap(a_bq), in1=Ct["dTi"][:, :],
                                                             op=ALU.mult), [qs_r[a_bq], Ct_r["dTi"]], [Ct_r["attnT"]])
            yield
            t_bq = pslot()
            S.op("pe", lambda e, t_bq=t_bq, csl=csl: e.transpose(out=psap(t_bq), in_=F["knT"][:, csl], identity=ident[:, :]),
                 [F_r["knT"], ident_r], [qs_r[t_bq]])
            yield
            evac(Ct["kd"][:, :], Ct_r["kd"], t_bq, [tm_r["ekd"]], scale=tm["ekd"][:, n, j:j + 1])
            yield
            t_bq = pslot()
            S.op("pe", lambda e, t_bq=t_bq, csl=csl: e.transpose(out=psap(t_bq), in_=F["vf"][:, csl], identity=ident[:, :]),
                 [F_r["vf"], ident_r], [qs_r[t_bq]])
            yield
            evac(Ct["vs"][:, :], Ct_r["vs"], t_bq, [tm_r["sqb"]], scale=tm["sqb"][:, n, j:j + 1])
            yield
            S.op("dve", lambda e: e.tensor_tensor(out=Ct["P"][:, :], in0=ident[:, :], in1=Ct["B0"][:, :], op=ALU.subtract),
                 [ident_r, Ct_r["B0"]], [Ct_r["P"]])
            yield
            Ak, Bk = "A0", "B0"
            for lvl in range(1, 7):
                An, Bn = ("A1", "B1") if Ak == "A0" else ("A0", "B0")
                bq = pslot()
                mm(bq, Ct[Bk][:, :], Ct[Ak][:, :], [Ct_r[Bk], Ct_r[Ak]])
                yield
                if lvl < 6:
                    bq2 = pslot()
                    mm(bq2, Ct[Ak][:, :], Ct[Bk][:, :], [Ct_r[Bk], Ct_r[Ak]])
                    yield
                evac(Ct[An][:, :], Ct_r[An], bq)
                yield
                if lvl < 6:
                    evac(Ct[Bn][:, :], Ct_r[Bn], bq2)
                    yield
                bq3 = pslot()
                mm(bq3, Ct[An][:, :], Ct["P"][:, :], [Ct_r[An], Ct_r["P"]])
                yield
                S.op("dve", lambda e, bq3=bq3: e.tensor_tensor(out=Ct["P"][:, :], in0=Ct["P"][:, :], in1=psap(bq3),
                                                               op=ALU.add), [qs_r[bq3]], [Ct_r["P"]])
                yield
                Ak, Bk = An, Bn


        def recur(n, bset):
            Ct, Ct_r = Cts[bset]
            csl = slice(n * 128, (n + 1) * 128)
            bq = pslot()
            mm(bq, F["knT"][:, csl], Sj, [F_r["knT"], Sj_r])
            yield
            S.op("dve", lambda e, bq=bq, n=n, j=j: e.scalar_tensor_tensor(
                out=Ct["Rp"][:, :], in0=psap(bq), scalar=tm["nbe"][:, n, j:j + 1], in1=Ct["vs"][:, :], op0=ALU.mult,
                op1=ALU.add), [qs_r[bq], tm_r["nbe"], Ct_r["vs"]], [Ct_r["Rp"]])
            yield
            bq = pslot()
            mm(bq, Ct["P"][:, :], Ct["Rp"][:, :], [Ct_r["P"], Ct_r["Rp"]])
            yield
            evac(Ct["vn"][:, :], Ct_r["vn"], bq, [tm_r["sqb"]], scale=tm["sqb"][:, n, j:j + 1])
            yield
            o_bq = pslot()
            mm(o_bq, Sj, Ct["qdT"][:, :], [Sj_r, Ct_r["qdT"]], start=True, stop=False)
            mm(o_bq, Ct["vn"][:, :], Ct["attnT"][:, :], [Ct_r["vn"], Ct_r["attnT"]], start=False, stop=True)
            yield
            evac(F["oT"][:, csl], F_r["oT"], o_bq)
            yield
            bq = pslot()
            mm(bq, Ct["kd"][:, :], Ct["vn"][:, :], [Ct_r["kd"], Ct_r["vn"]])
            yield
            S.op("dve", lambda e, bq=bq, n=n, j=j, Sj=Sj: e.scalar_tensor_tensor(
                out=Sj, in0=Sj, scalar=tm["egl"][:, n, j:j + 1], in1=psap(bq), op0=ALU.mult, op1=ALU.add),
                [qs_r[bq], tm_r["egl"]], [Sj_r])
            yield


        def interleave(gens):
            gens = [g for g in gens if g is not None]
            while gens:
                for g in list(gens):
                    try:
                        next(g)
                    except StopIteration:
                        gens.remove(g)

        def chain(*gs):
            for g in gs:
                yield from g

        interleave([prep(0, 0), prep(1, 1)])
        for m in range(NCH // 2):
            nxt = []
            if 2 * m + 2 < NCH:
                nxt = [prep(2 * m + 2, (2 * m + 2) % 4), prep(2 * m + 3, (2 * m + 3) % 4)]
            interleave([chain(recur(2 * m, (2 * m) % 4), recur(2 * m + 1, (2 * m + 1) % 4))] + nxt)
        S.op("act", lambda e: e.activation(out=F["t1"][:, :], in_=F["oT"][:, :], func=AF.Square), [F_r["oT"], F_r["t1"]],
             [F_r["t1"]])
        for half in range(2):
            S.op("pe", lambda e, half=half: e.matmul(c.ps[half][:, :], ones[:, :], F["t1"][:, half * 512:(half + 1) * 512],
                                                     start=True, stop=True), [ones_r, F_r["t1"]], [c.psr[half]])
        for half in range(2):
            S.op("dve", lambda e, half=half: e.scalar_tensor_tensor(
                out=F["t1"][:, half * 512:(half + 1) * 512], in0=c.ps[half][:, :], scalar=1.0 / 128.0,
                in1=F["epsq"][:, half * 512:(half + 1) * 512], op0=ALU.mult, op1=ALU.add),
                [c.psr[half], F_r["epsq"], F_r["t1"]], [F_r["t1"]])
        S.op("act", lambda e: e.activation(out=F["t1"][:, :], in_=F["t1"][:, :], func=AF.Sqrt), [F_r["t1"]], [F_r["t1"]])
        S.op("dve", lambda e: e.reciprocal(out=F["t1"][:, :], in_=F["t1"][:, :]), [F_r["t1"]], [F_r["t1"]])
        S.op("dve", lambda e: e.tensor_tensor(out=F["t1"][:, :], in0=F["t1"][:, :], in1=F["oT"][:, :], op=ALU.mult),
             [F_r["t1"], F_r["oT"]], [F_r["t1"]])
        S.op("act", lambda e: e.activation(out=F["oT"][:, :], in_=raw["z"][:, HALO:NT], func=AF.Silu),
             [raw_r["z"], F_r["t1"]], [F_r["oT"]])
        yb = j % 2
        S.op("dve", lambda e, yb=yb: e.scalar_tensor_tensor(out=yst[yb][:, :], in0=F["t1"][:, :], scalar=dnw[:, 0:1],
                                                            in1=F["oT"][:, :], op0=ALU.mult, op1=ALU.mult),
             [F_r["t1"], F_r["oT"], dnw_r], [yst_r[yb]])
        yr = Res("yscr")
        S.dma("sp", yscr[DL + j * 128:DL + (j + 1) * 128, :], yst[yb][:, :], [yst_r[yb]], [yr], yst_r[yb])
        yscr_rs.append(yr)


def phase_a_inputs(inp, b):
    f = lambda a: np.ascontiguousarray(a, dtype=np.float32)
    x = inp["x"]
    xe = np.zeros((NSEG * NTOK + HALO, D), np.float32)
    xe[HALO:] = x[b]
    lcw = np.stack([fm_vec(inp["lru_conv_w"][0, k]) for k in range(4)], axis=-1).reshape(128, NH * 4)
    dcw = np.stack([fm_vec(inp["dn_conv_w"][0, k]) for k in range(4)], axis=-1).reshape(128, 48 * 4)
    idx = np.arange(128)
    return {
        "xe": xe, "nwfm": fm_vec(inp["even_norm_w"][0]), "w_in": f(inp["even_w_in"][0]), "w_out": f(inp["even_w_out"][0]),
        "lcw": f(lcw), "lcb": fm_vec(inp["lru_conv_b"][0]), "lbr": fm_vec(inp["lru_b_r"][0]),
        "lbi": fm_vec(inp["lru_b_i"][0]), "lam": fm_vec(inp["lru_lambda"][0]),
        "wr": f(inp["lru_w_r"][0]), "wi": f(inp["lru_w_i"][0]), "dcw": f(dcw),
        "alog_b": f(np.broadcast_to(inp["dn_a_log"][0], (128, NH))), "dtb_b": f(np.broadcast_to(inp["dn_dt_bias"][0], (128, NH))),
        "dnw": f(inp["dn_norm_w"][0].reshape(128, 1)),
        "ident": np.eye(128, dtype=np.float32), "ones": np.ones((128, 128), np.float32),
        "ut": f(idx[:, None] <= idx[None, :]), "mls": f(idx[None, :] < idx[:, None]),
        "mus": f(idx[None, :] > idx[:, None]), "mui": f(idx[None, :] >= idx[:, None]),
    }


_CACHE = {}


def build_fused():
    nc = bass.Bass("TRN2", target_bir_lowering=False)
    TA = phase_a_decl(nc, "a_")
    TC = phase_c_decl(nc, "c_", with_x1e=False)
    msel_d = nc.dram_tensor("msel", [128, 4], F32, kind="ExternalInput").ap()
    zrow_d = nc.dram_tensor("zrow", [HALO, D], F32, kind="ExternalInput").ap()
    x1full = nc.dram_tensor("x1full", [NSEG * NTOK + HALO, D], F32, kind="Internal").ap()
    xown = nc.dram_tensor("xown", [NT, D], F32, kind="Internal").ap()
    TA["x1"] = x1full[HALO:, :]
    es = ExitStack()
    with es:
        c = setup_common(nc, es)
        S = c.S
        with ExitStack() as sa:
            zt = sa.enter_context(nc.sbuf_tensor("zt", [HALO, D], F32))
            zt_r = Res("zt")
            S.dma("sp", zt[:, :], zrow_d, [], [zt_r], zt_r)
            zr = Res("zrows")
            S.dma("sp", x1full[0:HALO, :], zt[:, :], [zt_r], [zr], zt_r)
            phase_a_body(c, sa, nc, TA)
            S.barrier()
        S.new_generation()
        with ExitStack() as sb:
            phase_c_body(c, sb, nc, TC, sel=(x1full, msel_d, xown))
        S.finalize()
    return nc


def kernel(x, even_norm_w, even_w_in, lru_conv_w, lru_conv_b, lru_w_r, lru_b_r, lru_w_i, lru_b_i, lru_lambda,
           dn_conv_w, dn_a_log, dn_dt_bias, dn_norm_w, even_w_out, odd_norm_w, odd_w_in, odd_conv_w, odd_w_out,
           final_norm_w):
    A = np.asarray
    inp = dict(x=A(x, np.float32), even_norm_w=A(even_norm_w), even_w_in=A(even_w_in),
               lru_conv_w=A(lru_conv_w), lru_conv_b=A(lru_conv_b), lru_w_r=A(lru_w_r),
               lru_b_r=A(lru_b_r), lru_w_i=A(lru_w_i), lru_b_i=A(lru_b_i),
               lru_lambda=A(lru_lambda), dn_conv_w=A(dn_conv_w), dn_a_log=A(dn_a_log),
               dn_dt_bias=A(dn_dt_bias), dn_norm_w=A(dn_norm_w), even_w_out=A(even_w_out))
    if "f" not in _CACHE:
        _CACHE["f"] = build_fused()
    odd_conv_w = A(odd_conv_w)
    cwl = np.ascontiguousarray(np.stack([fm_vec(odd_conv_w[0, k]) for k in range(3)], axis=-1).reshape(128, KC * 3))
    cmn = {
        "c_nwfm": fm_vec(A(odd_norm_w)[0]), "c_w_in": np.ascontiguousarray(A(odd_w_in)[0], dtype=np.float32), "c_cw": cwl,
        "c_w_out": np.ascontiguousarray(A(odd_w_out)[0], dtype=np.float32),
        "c_fnw": np.ascontiguousarray(np.broadcast_to(A(final_norm_w).astype(np.float32), (128, D))),
        "c_ident": np.eye(128, dtype=np.float32), "zrow": np.zeros((HALO, D), np.float32),
    }
    pa = [{"a_" + k: v for k, v in phase_a_inputs(inp, b).items()} for b in range(2)]
    in_maps = []
    for core in range(8):
        b, sq = divmod(core, 4)
        m = dict(cmn)
        m.update(pa[b])
        ms = np.zeros((128, 4), np.float32)
        ms[:, sq] = 1.0
        m["msel"] = ms
        in_maps.append(m)
    res = run_bass_kernel_spmd(_CACHE["f"], in_maps, core_ids=list(range(8)))
    out = np.zeros((2, NSEG * NTOK, D), np.float32)
    for core in range(8):
        b, sq = divmod(core, 4)
        out[b, sq * NTOK:(sq + 1) * NTOK] = res.results[core]["out"]
    return out
```

```python
import numpy as np
from contextlib import ExitStack
import concourse.bass as bass
import concourse.mybir as mybir
from concourse.bass_utils import run_bass_kernel_spmd

F32 = mybir.dt.float32
BF16 = mybir.dt.bfloat16
AF = mybir.ActivationFunctionType
ALU = mybir.AluOpType

D = 4096
NTOK = 1024
HALO = 4
NT = NTOK + HALO
KC = D // 128
EPS = 1e-6
TT = [(0, 344), (344, 342), (686, 342)]
OWT = 512
ENGS = ("pe", "act", "dve", "pool", "sp")


class Res:
    __slots__ = ("w", "rs", "name", "sem", "dcnt", "multi")

    def __init__(self, name="", multi=False):
        self.w = {}
        self.rs = {}
        self.name = name
        self.sem = None
        self.dcnt = 0
        self.multi = multi


def _merge(d, tok):
    k = id(tok[0])
    if k not in d or d[k][1] < tok[1]:
        d[k] = tok


class Sched:
    def __init__(self, nc, es):
        self.nc, self.es = nc, es
        self.prog = {e: [] for e in ENGS}
        self.psem, self.pcnt = {}, {}
        for e in ("pe", "act", "dve", "pool"):
            self.psem[e] = es.enter_context(nc.semaphore("prog_" + e))
            self.pcnt[e] = 0
        self.seen = {e: {} for e in ENGS}
        self.nsem = 0
        self.semkey = {}
        self.named = {}
        self.gen = 0
        self.pe_sems = {id(self.psem["pe"])}

    def _key(self, sem):
        k = id(sem)
        self.semkey[k] = sem
        return k

    def newsem(self, name):
        self.nsem += 1
        return self.es.enter_context(self.nc.semaphore("%s_%d" % (name, self.nsem)))

    def _deps(self, reads, writes):
        deps = []
        for r in reads:
            deps.extend(r.w.values())
        for r in writes:
            if not r.multi:
                deps.extend(r.w.values())
            deps.extend(r.rs.values())
        return deps

    def _waits(self, eng, deps):
        best = {}
        for (sem, val) in deps:
            k = self._key(sem)
            if eng == "pe" and id(sem) in self.pe_sems:
                continue
            if self.seen[eng].get(k, 0) >= val:
                continue
            if best.get(k, 0) < val:
                best[k] = val
        out = []
        for k, val in best.items():
            self.seen[eng][k] = val
            out.append((self.semkey[k], val))
        return out

    def _commit(self, tok, reads, writes):
        for r in reads:
            _merge(r.rs, tok)
        for r in writes:
            if r.multi:
                _merge(r.w, tok)
            else:
                r.w = {id(tok[0]): tok}
                r.rs = {}

    def op(self, eng, fn, reads=(), writes=()):
        waits = self._waits(eng, self._deps(reads, writes))
        self.pcnt[eng] += 1
        tok = (self.psem[eng], self.pcnt[eng])
        self.prog[eng].append((waits, fn, self.psem[eng], 1))
        self._commit(tok, reads, writes)
        return tok

    def dma(self, q, out, in_, reads, writes, semres):
        waits = self._waits(q, self._deps(reads, writes))
        rec = self.named.setdefault(semres.name, [None, 0])
        if rec[0] is None:
            rec[0] = self.newsem("d")
        rec[1] += 16
        tok = (rec[0], rec[1])
        self.prog[q].append((waits, lambda e: e.dma_start(out=out, in_=in_), rec[0], 16))
        self._commit(tok, reads, writes)
        return tok

    def new_generation(self):
        self.gen += 1
        for e in ("pe", "act", "dve", "pool"):
            self.psem[e] = self.es.enter_context(self.nc.semaphore("prog_%s_g%d" % (e, self.gen)))
            self.pcnt[e] = 0
            if e == "pe":
                self.pe_sems.add(id(self.psem[e]))

    def barrier(self):
        toks = [(self.psem[e], self.pcnt[e]) for e in self.psem if self.pcnt[e] > 0]
        toks += [(rec[0], rec[1]) for rec in self.named.values() if rec[1] > 0]
        for e in ENGS:
            waits = self._waits(e, toks)
            if waits:
                self.prog[e].append((waits, None, None, 0))

    def wait_all(self, eng, ress):
        deps = []
        for r in ress:
            deps.extend(r.w.values())
        waits = self._waits(eng, deps)
        self.prog[eng].append((waits, None, None, 0))

    def finalize(self):
        nc = self.nc
        block = self.es.enter_context(nc.Block())
        names = {"pe": "tensor", "act": "scalar", "dve": "vector", "pool": "gpsimd", "sp": "sync"}

        def mk(eng):
            def body(e):
                for (waits, fn, sem, n) in self.prog[eng]:
                    for (s, v) in waits:
                        e.wait_ge(s, v)
                    if fn is not None:
                        fn(e).then_inc(sem, n)
            return body

        for eng in ENGS:
            if self.prog[eng]:
                getattr(block, names[eng])(mk(eng))


class Ctx:
    pass


_UNIQ = [0]


def alloc(c, name, shape, dt):
    _UNIQ[0] += 1
    return c.es.enter_context(c.nc.sbuf_tensor("%s_u%d" % (name, _UNIQ[0]), shape, dt))


def setup_common(nc, es):
    c = Ctx()
    c.nc, c.es = nc, es
    c.S = Sched(nc, es)
    c.ps = [es.enter_context(nc.psum_tensor("psb%d" % i, [128, 512], F32)) for i in range(8)]
    c.psr = [Res("ps%d" % i) for i in range(8)]
    return c


def load_const(c, name, dram_ap, shape, dt=F32, q="pool"):
    t = alloc(c, name, shape, dt)
    r = Res(name)
    idx = tuple(slice(None) for _ in shape)
    c.S.dma(q, t[idx], dram_ap, [], [r], r)
    return t, r


def norm_to_hT(c, x_dram, nwfm, nwfm_r, ident, ident_r, hT, hT_r, scope, sel=None):
    S = c.S
    xt = [alloc(scope, "xt%d" % i, [128, D], F32) for i in range(2)]
    xt_r = [Res("xt%d" % i) for i in range(2)]
    xs = alloc(scope, "xs", [128, D], F32)
    xs_r = Res("xs")
    st = alloc(scope, "nstat", [128, 4], F32)
    st_r = Res("nstat")
    tiles = [(0, HALO)] + [(HALO + 128 * i, 128) for i in range(NTOK // 128)]
    for ti, (r0, n) in enumerate(tiles):
        b = ti % 2
        if sel is None:
            S.dma("sp", xt[b][0:n, :], x_dram[r0:r0 + n, :], [], [xt_r[b]], xt_r[b])
        else:
            xfull, msel, msel_r, xown, xq, xq_r, own_rs = sel
            for sq in range(4):
                qb = (ti * 4 + sq) % 2
                S.dma("sp", xq[qb][0:n, :], xfull[sq * NTOK + r0:sq * NTOK + r0 + n, :], [], [xq_r[qb]], xq_r[qb])
                if sq == 0:
                    S.op("dve", lambda e, b=b, n=n, qb=qb: e.tensor_scalar(out=xt[b][0:n, :], in0=xq[qb][0:n, :],
                                                                           scalar1=msel[0:n, 0:1], scalar2=None,
                                                                           op0=ALU.mult), [xq_r[qb], msel_r], [xt_r[b]])
                else:
                    S.op("dve", lambda e, b=b, n=n, qb=qb, sq=sq: e.scalar_tensor_tensor(
                        out=xt[b][0:n, :], in0=xq[qb][0:n, :], scalar=msel[0:n, sq:sq + 1], in1=xt[b][0:n, :],
                        op0=ALU.mult, op1=ALU.add), [xq_r[qb], msel_r], [xt_r[b]])
            orr = Res("xown")
            S.dma("sp", xown[r0:r0 + n, :], xt[b][0:n, :], [xt_r[b]], [orr], xt_r[b])
            own_rs.append(orr)
        S.op("act", lambda e, b=b, n=n: e.activation(out=xs[0:n, :], in_=xt[b][0:n, :], func=AF.Square,
                                                     accum_out=st[0:n, 0:1]),
             [xt_r[b]], [xs_r, st_r])
        S.op("dve", lambda e, n=n: e.tensor_scalar(out=st[0:n, 1:2], in0=st[0:n, 0:1], scalar1=1.0 / D, scalar2=EPS,
                                                   op0=ALU.mult, op1=ALU.add), [st_r], [st_r])
        S.op("act", lambda e, n=n: e.activation(out=st[0:n, 2:3], in_=st[0:n, 1:2], func=AF.Sqrt), [st_r], [st_r])
        S.op("dve", lambda e, n=n: e.reciprocal(out=st[0:n, 3:4], in_=st[0:n, 2:3]), [st_r], [st_r])
        S.op("dve", lambda e, b=b, n=n: e.tensor_scalar(out=xs[0:n, :], in0=xt[b][0:n, :], scalar1=st[0:n, 3:4],
                                                        scalar2=None, op0=ALU.mult), [xt_r[b], st_r], [xs_r])
        for kc in range(KC):
            pb = 4 + (kc // 4) % 4
            q = kc % 4
            S.op("pe", lambda e, kc=kc, pb=pb, q=q, n=n: e.transpose(out=c.ps[pb][:, q * 128:q * 128 + n],
                                                                      in_=xs[0:n, kc * 128:(kc + 1) * 128],
                                                                      identity=ident[0:n, 0:n]),
                 [xs_r, ident_r], [c.psr[pb]])
            if q == 3:
                for qq in range(4):
                    k2 = kc - 3 + qq
                    S.op("act", lambda e, k2=k2, pb=pb, qq=qq, n=n, r0=r0: e.activation(
                        out=hT[:, k2, r0:r0 + n], in_=c.ps[pb][:, qq * 128:qq * 128 + n], func=AF.Copy,
                        scale=nwfm[:, k2:k2 + 1]), [c.psr[pb], nwfm_r], [hT_r])


def proj_fm(c, wslot, wcol0, hT, hT_r, wr, bankset):
    S = c.S
    for kc in range(KC):
        for ti, (t0, tn) in enumerate(TT):
            pb = bankset[ti]
            S.op("pe", lambda e, kc=kc, pb=pb, t0=t0, tn=tn: e.matmul(
                c.ps[pb][:, 0:tn], wslot[:, kc, wcol0:wcol0 + 128], hT[:, kc, t0:t0 + tn],
                start=(kc == 0), stop=(kc == KC - 1)), [wr, hT_r], [c.psr[pb]])


def evac_fm(c, bankset, dst, dst_r, func=None, eng_rot=("act", "dve", "act")):
    S = c.S
    for ti, (t0, tn) in enumerate(TT):
        pb = bankset[ti]
        eng = eng_rot[ti] if func is None else "act"
        if eng == "act":
            f = AF.Copy if func is None else func
            S.op("act", lambda e, pb=pb, t0=t0, tn=tn, f=f: e.activation(out=dst[:, t0:t0 + tn], in_=c.ps[pb][:, 0:tn],
                                                                        func=f), [c.psr[pb]], [dst_r])
        else:
            S.op("dve", lambda e, pb=pb, t0=t0, tn=tn: e.tensor_copy(out=dst[:, t0:t0 + tn], in_=c.ps[pb][:, 0:tn]),
                 [c.psr[pb]], [dst_r])


def out_proj(c, ybig, ybig_r, w_dram, xres_dram, xres_row0, out_dram, scope, ssq=None, ssq_r=None):
    S = c.S
    WT = OWT
    nct = D // WT
    NWO = 2
    wsl = [alloc(scope, "wo%d" % i, [128, KC, WT], BF16) for i in range(NWO)]
    wsl_r = [Res("wo%d" % i) for i in range(NWO)]
    xr = [alloc(scope, "xr%d" % i, [128, WT], F32) for i in range(3)]
    xr_r = [Res("xr%d" % i) for i in range(3)]
    ot = [alloc(scope, "ot%d" % i, [128, WT], F32) for i in range(3)]
    ot_r = [Res("ot%d" % i) for i in range(3)]
    junk = alloc(scope, "ojunk", [128, WT], F32)
    junk_r = Res("ojunk")
    wv = w_dram.rearrange("(k p) n -> p k n", p=128)
    out_rs = []
    it = 0
    for ct in range(nct):
        ws = ct % NWO
        S.dma("pool", wsl[ws][:, :, :], wv[:, :, ct * WT:(ct + 1) * WT], [], [wsl_r[ws]], wsl_r[ws])
        for i in range(NTOK // 128):
            pb = it % 2
            sl = it % 3
            it += 1
            S.dma("sp", xr[sl][:, :], xres_dram[xres_row0 + i * 128:xres_row0 + (i + 1) * 128, ct * WT:(ct + 1) * WT],
                  [], [xr_r[sl]], xr_r[sl])
            for kc in range(KC):
                S.op("pe", lambda e, kc=kc, pb=pb, i=i, ws=ws: e.matmul(
                    c.ps[pb][:, 0:WT], ybig[:, kc, i * 128:(i + 1) * 128], wsl[ws][:, kc, :],
                    start=(kc == 0), stop=(kc == KC - 1)), [ybig_r, wsl_r[ws]], [c.psr[pb]])
            S.op("dve", lambda e, pb=pb, sl=sl: e.tensor_tensor(out=ot[sl][:, :], in0=c.ps[pb][:, 0:WT], in1=xr[sl][:, :],
                                                               op=ALU.add), [c.psr[pb], xr_r[sl]], [ot_r[sl]])
            if ssq is not None:
                S.op("act", lambda e, sl=sl, i=i, ct=ct: e.activation(out=junk[:, :], in_=ot[sl][:, :], func=AF.Square,
                                                                     accum_out=ssq[:, i * nct + ct:i * nct + ct + 1]),
                     [ot_r[sl]], [junk_r, ssq_r])
            orr = Res("o")
            S.dma("sp", out_dram[i * 128:(i + 1) * 128, ct * WT:(ct + 1) * WT], ot[sl][:, :], [ot_r[sl]], [orr], ot_r[sl])
            out_rs.append(orr)
    return out_rs


def phase_c_decl(nc, pfx="", with_x1e=True):
    di = lambda name, shape: nc.dram_tensor(pfx + name, shape, F32, kind="ExternalInput").ap()
    T = {}
    if with_x1e:
        T["x1e"] = di("x1e", [NT, D])
    T["nw_d"] = di("nwfm", [128, KC]); T["w_in"] = di("w_in", [D, 4 * D]); T["cw_d"] = di("cw", [128, KC * 3])
    T["w_out"] = di("w_out", [D, D]); T["fnw_d"] = di("fnw", [128, D]); T["ident_d"] = di("ident", [128, 128])
    T["out"] = nc.dram_tensor("out", [NTOK, D], F32, kind="ExternalOutput").ap()
    T["yscr"] = nc.dram_tensor(pfx + "yscr", [D, NTOK], BF16, kind="Internal").ap()
    return T


def build_phase_c(dbg=False):
    nc = bass.Bass("TRN2", target_bir_lowering=False)
    T = phase_c_decl(nc)
    es = ExitStack()
    with es:
        c = setup_common(nc, es)
        phase_c_body(c, es, nc, T)
        c.S.finalize()
    return nc


def phase_c_body(c0, es, nc, T, sel=None):
    dbg = False
    dbgo = None
    x1e, nw_d, w_in, cw_d, w_out, fnw_d, ident_d, out, yscr = (T.get("x1e"), T["nw_d"], T["w_in"], T["cw_d"], T["w_out"],
                                                               T["fnw_d"], T["ident_d"], T["out"], T["yscr"])
    if True:
        c = Ctx()
        c.nc, c.es, c.S, c.ps, c.psr = nc, es, c0.S, c0.ps, c0.psr
        S = c.S
        nwfm, nwfm_r = load_const(c, "nwfm_s", nw_d, [128, KC])
        cw, cw_r = load_const(c, "cw_s", cw_d, [128, KC * 3])
        ident, ident_r = load_const(c, "ident_s", ident_d, [128, 128])
        big = alloc(c, "big", [128, KC, NT], BF16)
        big_r = Res("big", multi=True)
        ssq = alloc(c, "ssq", [128, 8 * (D // OWT)], F32)
        ssq_r = Res("ssq", multi=True)
        with ExitStack() as s1:
            sc = Ctx(); sc.nc, sc.es = nc, s1
            if sel is not None:
                xfull, msel_d, xown = sel
                msel, msel_r = load_const(sc_as(c, sc), "msel_s", msel_d, [128, 4])
                xq = [alloc(sc, "xq%d" % i, [128, D], F32) for i in range(2)]
                xq_r = [Res("xq%d" % i) for i in range(2)]
                norm_to_hT(c, None, nwfm, nwfm_r, ident, ident_r, big, big_r, sc,
                           sel=(xfull, msel, msel_r, xown, xq, xq_r, []))
                x1e = xown
            else:
                norm_to_hT(c, x1e, nwfm, nwfm_r, ident, ident_r, big, big_r, sc)
            S.barrier()
        yscr_rs = []
        with ExitStack() as s2:
            sc = Ctx(); sc.nc, sc.es = nc, s2
            WT = 256
            NW = 4
            wsl = [alloc(sc, "wi%d" % i, [128, KC, WT], BF16) for i in range(NW)]
            wsl_r = [Res("wi%d" % i) for i in range(NW)]
            fm = {}
            fm_r = {}
            for hh in range(2):
                for nm in ("xin", "cc", "bb", "sz"):
                    fm[nm, hh] = alloc(sc, "fm_%s%d" % (nm, hh), [128, NT], F32)
                    fm_r[nm, hh] = Res("fm_%s%d" % (nm, hh), multi=True)
            cx = alloc(sc, "fm_cx", [128, NT], F32)
            cx_r = Res("fm_cx")
            cv = alloc(sc, "fm_cv", [128, NTOK], F32)
            cv_r = Res("fm_cv")
            yst = [alloc(sc, "yst%d" % i, [128, NTOK], BF16) for i in range(2)]
            yst_r = [Res("yst%d" % i) for i in range(2)]
            wv = w_in.rearrange("(k p) n -> p k n", p=128)
            wi = 0
            bs = 0
            for cp in range(D // WT):
                for g, nm, fn in ((0, "xin", None), (2, "cc", None), (1, "bb", None), (3, "sz", AF.Silu)):
                    sl = wi % NW
                    wi += 1
                    S.dma("pool", wsl[sl][:, :, :], wv[:, :, g * D + cp * WT:g * D + (cp + 1) * WT], [], [wsl_r[sl]],
                          wsl_r[sl])
                    for hh in range(2):
                        bankset = (0, 1, 2) if bs % 2 == 0 else (3, 4, 5)
                        bs += 1
                        proj_fm(c, wsl[sl], hh * 128, big, big_r, wsl_r[sl], bankset)
                        evac_fm(c, bankset, fm[nm, hh], fm_r[nm, hh], func=fn)
                for hh in range(2):
                    ch = cp * 2 + hh
                    S.op("dve", lambda e, hh=hh: e.tensor_tensor(out=cx[:, :], in0=fm["cc", hh][:, :],
                                                                 in1=fm["xin", hh][:, :], op=ALU.mult),
                         [fm_r["cc", hh], fm_r["xin", hh]], [cx_r])
                    S.op("dve", lambda e, ch=ch: e.tensor_scalar(out=cv[:, :], in0=cx[:, 2:2 + NTOK],
                                                                 scalar1=cw[:, ch * 3:ch * 3 + 1], scalar2=None,
                                                                 op0=ALU.mult), [cx_r, cw_r], [cv_r])
                    for k in (1, 2):
                        S.op("dve", lambda e, ch=ch, k=k: e.scalar_tensor_tensor(
                            out=cv[:, :], in0=cx[:, 2 + k:2 + k + NTOK],
                            scalar=cw[:, ch * 3 + k:ch * 3 + k + 1], in1=cv[:, :], op0=ALU.mult, op1=ALU.add),
                            [cx_r, cw_r], [cv_r])
                    S.op("dve", lambda e, hh=hh: e.tensor_tensor(out=cv[:, :], in0=cv[:, :],
                                                                 in1=fm["bb", hh][:, HALO:NT], op=ALU.mult),
                         [fm_r["bb", hh]], [cv_r])
                    yb = ch % 2
                    S.op("dve", lambda e, yb=yb, hh=hh: e.tensor_tensor(out=yst[yb][:, :], in0=cv[:, :],
                                                                        in1=fm["sz", hh][:, HALO:NT], op=ALU.mult),
                         [cv_r, fm_r["sz", hh]], [yst_r[yb]])
                    yr = Res("yscr")
                    S.dma("sp", yscr[ch * 128:(ch + 1) * 128, :], yst[yb][:, :], [yst_r[yb]], [yr], yst_r[yb])
                    yscr_rs.append(yr)
                    if dbg and ch == 0:
                        for di, nm in enumerate(("xin", "cc", "bb", "sz")):
                            yr = Res("dbg")
                            S.dma("pool", dbgo[:, di * NT:(di + 1) * NT], fm[nm, 0][:, :], [fm_r[nm, 0]], [yr], fm_r[nm, 0])
                            yscr_rs.append(yr)
                        yr = Res("dbg")
                        S.dma("pool", dbgo[:, 4 * NT:4 * NT + NTOK], cv[:, :], [cv_r], [yr], cv_r)
                        yscr_rs.append(yr)
                if dbg and cp == 0:
                    break
            S.barrier()
        with ExitStack() as s3:
            sc = Ctx(); sc.nc, sc.es = nc, s3
            nct = D // OWT
            if dbg:
                S.wait_all("pool", yscr_rs)
                S.wait_all("sp", yscr_rs)
                o_rs = []
            else:
                S.dma("sp", big[:, :, 0:NTOK], yscr.rearrange("(k p) t -> p k t", p=128), yscr_rs, [big_r], big_r)
                o_rs = out_proj(c, big, big_r, w_out, x1e, HALO, out, sc, ssq, ssq_r)
            S.barrier()
        with ExitStack() as s4:
            sc = Ctx(); sc.nc, sc.es = nc, s4
            fnw, fnw_r = load_const(sc_as(c, sc), "fnw_s", fnw_d, [128, D])
            xt = [alloc(sc, "fx%d" % i, [128, D], F32) for i in range(2)]
            xt_r = [Res("fx%d" % i) for i in range(2)]
            st = alloc(sc, "fst", [128, 4], F32)
            st_r = Res("fst")
            fin = []
            if dbg:
                S.wait_all("sp", o_rs)
            for i in range(0 if dbg else 8):
                b = i % 2
                S.dma("sp", xt[b][:, :], out[i * 128:(i + 1) * 128, :], o_rs, [xt_r[b]], xt_r[b])
                S.op("dve", lambda e, i=i: e.tensor_reduce(out=st[:, 0:1], in_=ssq[:, i * nct:(i + 1) * nct],
                                                           axis=mybir.AxisListType.X, op=ALU.add), [ssq_r], [st_r])
                S.op("dve", lambda e: e.tensor_scalar(out=st[:, 1:2], in0=st[:, 0:1], scalar1=1.0 / D, scalar2=EPS,
                                                      op0=ALU.mult, op1=ALU.add), [st_r], [st_r])
                S.op("act", lambda e: e.activation(out=st[:, 2:3], in_=st[:, 1:2], func=AF.Sqrt), [st_r], [st_r])
                S.op("dve", lambda e: e.reciprocal(out=st[:, 3:4], in_=st[:, 2:3]), [st_r], [st_r])
                S.op("dve", lambda e, b=b: e.scalar_tensor_tensor(out=xt[b][:, :], in0=xt[b][:, :], scalar=st[:, 3:4],
                                                                  in1=fnw[:, :], op0=ALU.mult, op1=ALU.mult),
                     [st_r, fnw_r], [xt_r[b]])
                fr = Res("fin")
                S.dma("sp", out[i * 128:(i + 1) * 128, :], xt[b][:, :], [xt_r[b]], [fr], xt_r[b])
                fin.append(fr)
            S.wait_all("sp", fin)


def sc_as(c, sc):
    n = Ctx()
    n.nc, n.es, n.S = c.nc, sc.es, c.S
    return n


def fm_vec(v):
    v = np.ascontiguousarray(v, dtype=np.float32)
    return np.ascontiguousarray(v.reshape(-1, 128).T)


def with_halo(xfull, b, s):
    o = np.zeros((NT, xfull.shape[-1]), np.float32)
    o[HALO:] = xfull[b, s * NTOK:(s + 1) * NTOK]
    if s > 0:
        o[:HALO] = xfull[b, s * NTOK - HALO:s * NTOK]
    return o


def run_phase_c(x1, odd_norm_w, odd_w_in, odd_conv_w, odd_w_out, final_norm_w, trace=False, dbg=False):
    nc = build_phase_c(dbg)
    cwl = np.ascontiguousarray(np.stack([fm_vec(odd_conv_w[0, k]) for k in range(3)], axis=-1).reshape(128, KC * 3))
    common = {
        "nwfm": fm_vec(odd_norm_w[0]), "w_in": np.ascontiguousarray(odd_w_in[0]), "cw": cwl,
        "w_out": np.ascontiguousarray(odd_w_out[0]),
        "fnw": np.ascontiguousarray(np.broadcast_to(final_norm_w.astype(np.float32), (128, D))),
        "ident": np.eye(128, dtype=np.float32),
    }
    in_maps = []
    for core in range(8):
        b, s = divmod(core, 4)
        m = dict(common)
        m["x1e"] = with_halo(x1, b, s)
        in_maps.append(m)
    res = run_bass_kernel_spmd(nc, in_maps, core_ids=list(range(8)), trace=trace)
    out = np.zeros((2, 4 * NTOK, D), np.float32)
    for core in range(8):
        b, s = divmod(core, 4)
        out[b, s * NTOK:(s + 1) * NTOK] = res.results[core]["out"]
    return out, res


NSEG = 4
DL = 2048
NH = 16
DIN0 = 12320


def phase_a_decl(nc, pfx=""):
    di = lambda name, shape: nc.dram_tensor(pfx + name, shape, F32, kind="ExternalInput").ap()
    T = {}
    T["xe"] = di("xe", [NSEG * NTOK + HALO, D])
    T["nw_d"] = di("nwfm", [128, KC])
    T["w_in"] = di("w_in", [D, DIN0])
    T["w_out"] = di("w_out", [D, D])
    T["lcw_d"] = di("lcw", [128, NH * 4]); T["lcb_d"] = di("lcb", [128, NH])
    T["lbr_d"] = di("lbr", [128, NH]); T["lbi_d"] = di("lbi", [128, NH]); T["lam_d"] = di("lam", [128, NH])
    T["wr_d"] = di("wr", [NH, 128, 128]); T["wi_d"] = di("wi", [NH, 128, 128])
    T["dcw_d"] = di("dcw", [128, 48 * 4])
    T["alog_d"] = di("alog_b", [128, NH]); T["dtb_d"] = di("dtb_b", [128, NH]); T["dnw_d"] = di("dnw", [128, 1])
    T["ident_d"] = di("ident", [128, 128]); T["ones_d"] = di("ones", [128, 128]); T["ut_d"] = di("ut", [128, 128])
    T["mls_d"] = di("mls", [128, 128]); T["mus_d"] = di("mus", [128, 128]); T["mui_d"] = di("mui", [128, 128])
    T["yscr"] = nc.dram_tensor(pfx + "yscr", [D, NTOK], BF16, kind="Internal").ap()
    return T


def build_phase_a(nseg=NSEG, dbg=False, stop=None):
    nc = bass.Bass("TRN2", target_bir_lowering=False)
    T = phase_a_decl(nc)
    T["x1"] = nc.dram_tensor("x1", [NSEG * NTOK, D], F32, kind="ExternalOutput").ap()
    es = ExitStack()
    with es:
        c = setup_common(nc, es)
        phase_a_body(c, es, nc, T, nseg)
        c.S.finalize()
    return nc


def phase_a_body(c0, es, nc, T, nseg=NSEG, stop=None):
    (xe, nw_d, w_in, w_out, lcw_d, lcb_d, lbr_d, lbi_d, lam_d, wr_d, wi_d, dcw_d, alog_d, dtb_d, dnw_d, ident_d, ones_d,
     ut_d, mls_d, mus_d, mui_d, yscr, x1) = [T[k] for k in (
        "xe", "nw_d", "w_in", "w_out", "lcw_d", "lcb_d", "lbr_d", "lbi_d", "lam_d", "wr_d", "wi_d", "dcw_d", "alog_d",
        "dtb_d", "dnw_d", "ident_d", "ones_d", "ut_d", "mls_d", "mus_d", "mui_d", "yscr", "x1")]
    if True:
        c = Ctx()
        c.nc, c.es, c.S, c.ps, c.psr = nc, es, c0.S, c0.ps, c0.psr
        S = c.S
        nwfm, nwfm_r = load_const(c, "nwfm_s", nw_d, [128, KC])
        ident, ident_r = load_const(c, "ident_s", ident_d, [128, 128])
        ones, ones_r = load_const(c, "ones_s", ones_d, [128, 128])
        ut, ut_r = load_const(c, "ut_s", ut_d, [128, 128])
        mls, mls_r = load_const(c, "mls_s", mls_d, [128, 128])
        mus, mus_r = load_const(c, "mus_s", mus_d, [128, 128])
        mui, mui_r = load_const(c, "mui_s", mui_d, [128, 128])
        lcw, lcw_r = load_const(c, "lcw_s", lcw_d, [128, NH * 4])
        lcb, lcb_r = load_const(c, "lcb_s", lcb_d, [128, NH])
        lbr, lbr_r = load_const(c, "lbr_s", lbr_d, [128, NH])
        lbi, lbi_r = load_const(c, "lbi_s", lbi_d, [128, NH])
        lam, lam_r = load_const(c, "lam_s", lam_d, [128, NH])
        dcw, dcw_r = load_const(c, "dcw_s", dcw_d, [128, 48 * 4])
        alog, alog_r = load_const(c, "alog_s", alog_d, [128, NH])
        dtb, dtb_r = load_const(c, "dtb_s", dtb_d, [128, NH])
        dnw, dnw_r = load_const(c, "dnw_s", dnw_d, [128, 1])
        hst = alloc(c, "hst", [128, NH], F32); hst_r = Res("hst")
        Sst = alloc(c, "Sst", [128, NH, 128], F32); Sst_r = [Res("S%d" % j) for j in range(NH)]
        cc1 = alloc(c, "cc1", [128, NH], F32); cc1_r = Res("cc1")
        cc2 = alloc(c, "cc2", [128, NH], F32); cc2_r = Res("cc2")
        negA = alloc(c, "negA", [128, NH], F32); negA_r = Res("negA")
        S.op("dve", lambda e: e.memset(hst[:, :], 0.0), [], [hst_r])
        S.op("dve", lambda e: e.memset(Sst[:, :, :], 0.0), [], Sst_r)
        S.op("act", lambda e: e.activation(out=cc1[:, :], in_=lam[:, :], func=AF.Exp, scale=-1.0), [lam_r], [cc1_r])
        S.op("act", lambda e: e.activation(out=cc1[:, :], in_=cc1[:, :], func=AF.Ln, bias=1.0), [], [cc1_r])
        S.op("dve", lambda e: e.tensor_scalar(out=cc2[:, :], in0=cc1[:, :], scalar1=-16.0, scalar2=None, op0=ALU.mult),
             [cc1_r], [cc2_r])
        S.op("dve", lambda e: e.tensor_scalar(out=cc1[:, :], in0=cc1[:, :], scalar1=-8.0, scalar2=None, op0=ALU.mult),
             [cc2_r], [cc1_r])
        S.op("act", lambda e: e.activation(out=negA[:, :], in_=alog[:, :], func=AF.Exp), [alog_r], [negA_r])
        S.op("dve", lambda e: e.tensor_scalar(out=negA[:, :], in0=negA[:, :], scalar1=-1.0, scalar2=None, op0=ALU.mult),
             [], [negA_r])
        big = alloc(c, "big", [128, KC, NT], BF16)
        big_r = Res("big", multi=True)
        wv = w_in.rearrange("(k p) n -> p k n", p=128)
        bsn = [0]

        def nextbanks():
            bsn[0] += 1
            return (0, 1, 2) if bsn[0] % 2 == 0 else (3, 4, 5)

        for seg in range(nseg):
            if seg > 0:
                S.new_generation()
            with ExitStack() as s1:
                sc = Ctx(); sc.nc, sc.es = nc, s1
                norm_to_hT(c, xe[seg * NTOK:seg * NTOK + NT, :], nwfm, nwfm_r, ident, ident_r, big, big_r, sc)
                S.barrier()
            yscr_rs = []
            with ExitStack() as s2:
                sc = Ctx(); sc.nc, sc.es = nc, s2
                WT = 256
                wrp = [alloc(sc, "wr_s%d" % i, [128, 2, 128], F32) for i in range(2)]
                wrp_r = [Res("wr%d" % i) for i in range(2)]
                wip = [alloc(sc, "wi_s%d" % i, [128, 2, 128], F32) for i in range(2)]
                wip_r = [Res("wi%d" % i) for i in range(2)]
                wsl = [alloc(sc, "wl%d" % i, [128, KC, WT], BF16) for i in range(3)]
                wsl_r = [Res("wl%d" % i) for i in range(3)]
                raw = {}; raw_r = {}
                for par in range(2):
                    for g in range(2):
                        for hh in range(2):
                            raw[g, hh, par] = alloc(sc, "lraw%d%d%d" % (g, hh, par), [128, NT], F32)
                            raw_r[g, hh, par] = Res("lraw", multi=True)
                T = {}; T_r = {}
                for nm in ("xc", "r", "gi", "a", "t", "h"):
                    T[nm] = alloc(sc, "l_" + nm, [128, NTOK], F32)
                    T_r[nm] = Res("l_" + nm, multi=(nm in ("r", "gi")))
                yst = [alloc(sc, "lyst%d" % i, [128, NTOK], BF16) for i in range(2)]
                yst_r = [Res("lyst%d" % i) for i in range(2)]
                wn = 0
                for cp in range(NH // 2):
                    wr, wr_r, wi, wi_r = wrp[cp % 2], wrp_r[cp % 2], wip[cp % 2], wip_r[cp % 2]
                    S.dma("pool", wr[:, :, :], wr_d[cp * 2:cp * 2 + 2].rearrange("h i j -> i h j"), [], [wr_r], wr_r)
                    S.dma("pool", wi[:, :, :], wi_d[cp * 2:cp * 2 + 2].rearrange("h i j -> i h j"), [], [wi_r], wi_r)
                    for g in range(2):
                        sl = wn % 3
                        wn += 1
                        S.dma("pool", wsl[sl][:, :, :], wv[:, :, g * DL + cp * WT:g * DL + (cp + 1) * WT], [],
                              [wsl_r[sl]], wsl_r[sl])
                        for hh in range(2):
                            bk = nextbanks()
                            proj_fm(c, wsl[sl], hh * 128, big, big_r, wsl_r[sl], bk)
                            evac_fm(c, bk, raw[g, hh, cp % 2], raw_r[g, hh, cp % 2])
                    for hh in range(2):
                        j = cp * 2 + hh
                        xa, xa_r = raw[0, hh, cp % 2], raw_r[0, hh, cp % 2]
                        ga, ga_r = raw[1, hh, cp % 2], raw_r[1, hh, cp % 2]
                        S.op("dve", lambda e, j=j, xa=xa: e.tensor_scalar(
                            out=T["xc"][:, :], in0=xa[:, 1:1 + NTOK], scalar1=lcw[:, j * 4:j * 4 + 1],
                            scalar2=lcb[:, j:j + 1], op0=ALU.mult, op1=ALU.add), [xa_r, lcw_r, lcb_r], [T_r["xc"]])
                        for k in (1, 2, 3):
                            S.op("dve", lambda e, j=j, k=k, xa=xa: e.scalar_tensor_tensor(
                                out=T["xc"][:, :], in0=xa[:, 1 + k:1 + k + NTOK], scalar=lcw[:, j * 4 + k:j * 4 + k + 1],
                                in1=T["xc"][:, :], op0=ALU.mult, op1=ALU.add), [xa_r, lcw_r], [T_r["xc"]])
                        for half in range(2):
                            for (wt_, wt_r_, bias_, nm) in ((wr, wr_r, lbr, "r"), (wi, wi_r, lbi, "gi")):
                                pb = 6 + (half + (nm == "gi")) % 2
                                S.op("pe", lambda e, hh=hh, half=half, wt_=wt_, pb=pb: e.matmul(
                                    c.ps[pb][:, :], wt_[:, hh, :], T["xc"][:, half * 512:(half + 1) * 512],
                                    start=True, stop=True), [wt_r_, T_r["xc"]], [c.psr[pb]])
                                S.op("act", lambda e, j=j, half=half, bias_=bias_, nm=nm, pb=pb: e.activation(
                                    out=T[nm][:, half * 512:(half + 1) * 512], in_=c.ps[pb][:, :], func=AF.Sigmoid,
                                    bias=bias_[:, j:j + 1]), [c.psr[pb], lbr_r, lbi_r], [T_r[nm]])
                        S.op("act", lambda e, j=j: e.activation(out=T["a"][:, :], in_=T["r"][:, :], func=AF.Exp,
                                                                scale=cc1[:, j:j + 1]), [T_r["r"], cc1_r], [T_r["a"]])
                        S.op("act", lambda e, j=j: e.activation(out=T["t"][:, :], in_=T["r"][:, :], func=AF.Exp,
                                                                scale=cc2[:, j:j + 1]), [T_r["r"], cc2_r], [T_r["t"]])
                        S.op("dve", lambda e: e.tensor_scalar(out=T["t"][:, :], in0=T["t"][:, :], scalar1=-1.0, scalar2=1.0,
                                                              op0=ALU.mult, op1=ALU.add), [], [T_r["t"]])
                        S.op("act", lambda e: e.activation(out=T["t"][:, :], in_=T["t"][:, :], func=AF.Sqrt), [], [T_r["t"]])
                        S.op("dve", lambda e: e.tensor_tensor(out=T["gi"][:, :], in0=T["gi"][:, :], in1=T["xc"][:, :],
                                                              op=ALU.mult), [T_r["xc"], T_r["gi"]], [T_r["gi"]])
                        S.op("dve", lambda e: e.tensor_tensor(out=T["t"][:, :], in0=T["t"][:, :], in1=T["gi"][:, :],
                                                              op=ALU.mult), [T_r["gi"]], [T_r["t"]])
                        S.op("dve", lambda e, j=j: e.tensor_tensor_scan(out=T["h"][:, :], data0=T["a"][:, :],
                                                                        data1=T["t"][:, :], initial=hst[:, j:j + 1],
                                                                        op0=ALU.mult, op1=ALU.add),
                             [T_r["a"], T_r["t"], hst_r], [T_r["h"]])
                        S.op("dve", lambda e, j=j: e.tensor_copy(out=hst[:, j:j + 1], in_=T["h"][:, NTOK - 1:NTOK]),
                             [T_r["h"]], [hst_r])
                        S.op("act", lambda e, ga=ga: e.activation(out=T["xc"][:, :], in_=ga[:, HALO:NT], func=AF.Silu),
                             [ga_r], [T_r["xc"]])
                        yb = j % 2
                        S.op("dve", lambda e, yb=yb: e.tensor_tensor(out=yst[yb][:, :], in0=T["h"][:, :], in1=T["xc"][:, :],
                                                                     op=ALU.mult), [T_r["h"], T_r["xc"]], [yst_r[yb]])
                        yr = Res("yscr")
                        S.dma("sp", yscr[j * 128:(j + 1) * 128, :], yst[yb][:, :], [yst_r[yb]], [yr], yst_r[yb])
                        yscr_rs.append(yr)
                S.barrier()
            if stop == "lru":
                S.wait_all("sp", yscr_rs)
                break
            with ExitStack() as s3:
                sc = Ctx(); sc.nc, sc.es = nc, s3
                dn_segment(c, sc, big, big_r, wv, yscr, yscr_rs, Sst, Sst_r, dcw, dcw_r, negA, negA_r, dtb, dtb_r,
                           dnw, dnw_r, ident, ident_r, ones, ones_r, ut, ut_r, mls, mls_r, mus, mus_r, mui, mui_r,
                           nextbanks, stop)
                S.barrier()
            if stop in ("dnpro", "dn"):
                S.wait_all("sp", yscr_rs)
                break
            with ExitStack() as s4:
                sc = Ctx(); sc.nc, sc.es = nc, s4
                S.dma("sp", big[:, :, 0:NTOK], yscr.rearrange("(k p) t -> p k t", p=128), yscr_rs, [big_r], big_r)
                o_rs = out_proj(c, big, big_r, w_out, xe, HALO + seg * NTOK, x1[seg * NTOK:(seg + 1) * NTOK, :], sc)
                S.wait_all("sp", o_rs)
                S.barrier()


def dn_segment(c, sc, big, big_r, wv, yscr, yscr_rs, Sst, Sst_r, dcw, dcw_r, negA, negA_r, dtb, dtb_r, dnw, dnw_r,
               ident, ident_r, ones, ones_r, ut, ut_r, mls, mls_r, mus, mus_r, mui, mui_r, nextbanks, stop=None):
    S = c.S
    NCH = NTOK // 128
    wba = alloc(sc, "wba", [128, KC, 32], BF16); wba_r = Res("wba")
    S.dma("pool", wba[:, :, :], wv[:, :, 6 * DL:6 * DL + 32], [], [wba_r], wba_r)
    tm = {}
    tm_r = {}
    for nm in ("sqb", "g", "gc", "ngc", "nbe", "ekd", "egl", "tmp"):
        tm[nm] = alloc(sc, "tm_" + nm, [128, NCH, NH], F32)
        tm_r[nm] = Res("tm_" + nm)
    for i in range(NCH):
        for kc in range(KC):
            S.op("pe", lambda e, kc=kc, i=i: e.matmul(c.ps[6][:, 0:32], big[:, kc, HALO + i * 128:HALO + (i + 1) * 128],
                                                      wba[:, kc, :], start=(kc == 0), stop=(kc == KC - 1)),
                 [big_r, wba_r], [c.psr[6]])
        S.op("act", lambda e, i=i: e.activation(out=tm["sqb"][:, i, :], in_=c.ps[6][:, 0:16], func=AF.Sigmoid),
             [c.psr[6]], [tm_r["sqb"]])
        S.op("act", lambda e, i=i: e.activation(out=tm["sqb"][:, i, :], in_=tm["sqb"][:, i, :], func=AF.Sqrt),
             [], [tm_r["sqb"]])
        S.op("dve", lambda e, i=i: e.tensor_tensor(out=tm["g"][:, i, :], in0=c.ps[6][:, 16:32], in1=dtb[:, :], op=ALU.add),
             [c.psr[6], dtb_r], [tm_r["g"]])
        S.op("act", lambda e, i=i: e.activation(out=tm["g"][:, i, :], in_=tm["g"][:, i, :], func=AF.Exp), [], [tm_r["g"]])
        S.op("act", lambda e, i=i: e.activation(out=tm["g"][:, i, :], in_=tm["g"][:, i, :], func=AF.Ln, bias=1.0),
             [], [tm_r["g"]])
        S.op("dve", lambda e, i=i: e.tensor_tensor(out=tm["g"][:, i, :], in0=tm["g"][:, i, :], in1=negA[:, :], op=ALU.mult),
             [negA_r], [tm_r["g"]])
        S.op("pe", lambda e, i=i: e.matmul(c.ps[7][:, 0:16], ut[:, :], tm["g"][:, i, :], start=True, stop=True),
             [ut_r, tm_r["g"]], [c.psr[7]])
        S.op("pe", lambda e, i=i: e.matmul(c.ps[7][:, 16:32], ones[:, :], tm["g"][:, i, :], start=True, stop=True),
             [ones_r, tm_r["g"]], [c.psr[7]])
        S.op("dve", lambda e, i=i: e.tensor_copy(out=tm["gc"][:, i, :], in_=c.ps[7][:, 0:16]), [c.psr[7]], [tm_r["gc"]])
        S.op("dve", lambda e, i=i: e.tensor_scalar(out=tm["ngc"][:, i, :], in0=c.ps[7][:, 0:16], scalar1=-1.0, scalar2=None,
                                                   op0=ALU.mult), [c.psr[7]], [tm_r["ngc"]])
        S.op("act", lambda e, i=i: e.activation(out=tm["egl"][:, i, :], in_=c.ps[7][:, 16:32], func=AF.Exp),
             [c.psr[7]], [tm_r["egl"]])
        S.op("dve", lambda e, i=i: e.tensor_tensor(out=tm["tmp"][:, i, :], in0=c.ps[7][:, 16:32], in1=tm["gc"][:, i, :],
                                                   op=ALU.subtract), [c.psr[7], tm_r["gc"]], [tm_r["tmp"]])
        S.op("act", lambda e, i=i: e.activation(out=tm["ekd"][:, i, :], in_=tm["tmp"][:, i, :], func=AF.Exp),
             [tm_r["tmp"]], [tm_r["ekd"]])
        S.op("act", lambda e, i=i: e.activation(out=tm["tmp"][:, i, :], in_=tm["gc"][:, i, :], func=AF.Exp),
             [tm_r["gc"], tm_r["ekd"]], [tm_r["tmp"]])
        S.op("dve", lambda e, i=i: e.scalar_tensor_tensor(out=tm["nbe"][:, i, :], in0=tm["tmp"][:, i, :], scalar=-1.0,
                                                          in1=tm["sqb"][:, i, :], op0=ALU.mult, op1=ALU.mult),
             [tm_r["tmp"], tm_r["sqb"]], [tm_r["nbe"]])
    S.barrier()
    if stop == "dnpro":
        return
    WT = 128
    NWS = 4
    wsl = [alloc(sc, "wd%d" % i, [128, KC, WT], BF16) for i in range(NWS)]
    wsl_r = [Res("wd%d" % i) for i in range(NWS)]
    raw = {}; raw_r = {}
    for nm in ("q", "k", "v", "z"):
        raw[nm] = alloc(sc, "draw_" + nm, [128, NT], F32)
        raw_r[nm] = Res("draw_" + nm, multi=True)
    F = {}; F_r = {}
    for nm in ("qf", "kf", "vf", "knT", "t1", "epsq", "oT"):
        F[nm] = alloc(sc, "df_" + nm, [128, NTOK], F32)
        F_r[nm] = Res("df_" + nm, multi=(nm in ("t1", "epsq", "oT")))
    names = ["rl", "rhsb", "egc", "decs", "dTs", "dTi", "ktT", "qdT", "A0", "B0", "A1", "B1", "P", "attnT", "kd", "vs", "Rp", "vn"]
    outs_ = ("P", "attnT", "kd", "vs", "qdT")
    pool_t = {}
    for nm in names:
        depth = 4 if nm in outs_ else 2
        for k in range(depth):
            pool_t[nm, k] = (alloc(sc, "dc%d_%s" % (k, nm), [128, 256 if nm == "rhsb" else 128], F32),
                             Res("dc%d_%s" % (k, nm)))
    Cts = []
    for bset in range(4):
        Ct = {}; Ct_r = {}
        for nm in names:
            k = bset if nm in outs_ else bset % 2
            Ct[nm], Ct_r[nm] = pool_t[nm, k]
        Cts.append((Ct, Ct_r))
    yst = [alloc(sc, "dyst%d" % i, [128, NTOK], BF16) for i in range(2)]
    yst_r = [Res("dyst%d" % i) for i in range(2)]
    banks = [2, 3, 4, 7]
    qs_r = {b: c.psr[b] for b in range(8)}
    rot = [0]

    def pslot():
        rot[0] = (rot[0] + 1) % len(banks)
        return banks[rot[0]]

    def psap(b, n=128):
        return c.ps[b][:, 0:n]

    cnt = [0]

    def evac(dst, dst_r, src_bq, extra_reads=(), scale=None):
        cnt[0] += 1
        if scale is not None:
            S.op("act", lambda e: e.activation(out=dst, in_=psap(src_bq), func=AF.Copy, scale=scale),
                 [qs_r[src_bq]] + list(extra_reads), [dst_r])
        elif cnt[0] % 2 == 0:
            S.op("act", lambda e: e.activation(out=dst, in_=psap(src_bq), func=AF.Copy), [qs_r[src_bq]], [dst_r])
        else:
            S.op("dve", lambda e: e.tensor_copy(out=dst, in_=psap(src_bq)), [qs_r[src_bq]], [dst_r])

    def mm(bq, lhsT, rhs, reads, start=True, stop=True, n=128):
        S.op("pe", lambda e: e.matmul(psap(bq, n), lhsT, rhs, start=start, stop=stop), reads, [qs_r[bq]])

    wn = 0
    for j in range(NH):
        cols = {"q": 2 * DL + j * 128, "k": 3 * DL + j * 128, "v": 4 * DL + j * 128, "z": 5 * DL + j * 128}
        for nm in ("q", "k", "v", "z"):
            sl = wn % NWS
            wn += 1
            S.dma("pool", wsl[sl][:, :, :], wv[:, :, cols[nm]:cols[nm] + WT], [], [wsl_r[sl]], wsl_r[sl])
            bk = nextbanks()
            proj_fm(c, wsl[sl], 0, big, big_r, wsl_r[sl], bk)
            evac_fm(c, bk, raw[nm], raw_r[nm])
        for kind, (nm, dst) in enumerate((("q", "qf"), ("k", "kf"), ("v", "vf"))):
            ch = kind * NH + j
            S.op("dve", lambda e, nm=nm, ch=ch: e.tensor_scalar(out=F["t1"][:, :], in0=raw[nm][:, 1:1 + NTOK],
                                                                scalar1=dcw[:, ch * 4:ch * 4 + 1], scalar2=None,
                                                                op0=ALU.mult), [raw_r[nm], dcw_r], [F_r["t1"]])
            for k in (1, 2, 3):
                S.op("dve", lambda e, nm=nm, ch=ch, k=k: e.scalar_tensor_tensor(
                    out=F["t1"][:, :], in0=raw[nm][:, 1 + k:1 + k + NTOK], scalar=dcw[:, ch * 4 + k:ch * 4 + k + 1],
                    in1=F["t1"][:, :], op0=ALU.mult, op1=ALU.add), [raw_r[nm], dcw_r, F_r["t1"]], [F_r["t1"]])
            S.op("act", lambda e, dst=dst: e.activation(out=F[dst][:, :], in_=F["t1"][:, :], func=AF.Silu),
                 [F_r["t1"]], [F_r[dst]])
        for (src, which) in (("kf", "k"), ("qf", "q")):
            S.op("act", lambda e, src=src: e.activation(out=F["t1"][:, :], in_=F[src][:, :], func=AF.Square),
                 [F_r[src], F_r["t1"]], [F_r["t1"]])
            for half in range(2):
                S.op("pe", lambda e, half=half: e.matmul(c.ps[half][:, :], ones[:, :], F["t1"][:, half * 512:(half + 1) * 512],
                                                         start=True, stop=True), [ones_r, F_r["t1"]], [c.psr[half]])
            if which == "k":
                for half in range(2):
                    S.op("act", lambda e, half=half: e.activation(out=F["knT"][:, half * 512:(half + 1) * 512],
                                                                  in_=c.ps[half][:, :], func=AF.Ln, bias=EPS),
                         [c.psr[half]], [F_r["knT"]])
                S.op("act", lambda e: e.activation(out=F["knT"][:, :], in_=F["knT"][:, :], func=AF.Exp, scale=-0.5),
                     [], [F_r["knT"]])
                S.op("dve", lambda e: e.tensor_tensor(out=F["knT"][:, :], in0=F["knT"][:, :], in1=F["kf"][:, :], op=ALU.mult),
                     [F_r["kf"]], [F_r["knT"]])
            else:
                for half in range(2):
                    S.op("dve", lambda e, half=half: e.tensor_scalar(out=F["epsq"][:, half * 512:(half + 1) * 512],
                                                                     in0=c.ps[half][:, :], scalar1=EPS, scalar2=128.0 * EPS,
                                                                     op0=ALU.add, op1=ALU.mult),
                         [c.psr[half], F_r["epsq"]], [F_r["epsq"]])
        Sj = Sst[:, j, :]
        Sj_r = Sst_r[j]

        def prep(n, bset):
            Ct, Ct_r = Cts[bset]
            csl = slice(n * 128, (n + 1) * 128)
            pbk = 6 if n % 2 == 0 else 5
            S.op("dve", lambda e, n=n, j=j: e.tensor_scalar(out=Ct["rhsb"][:, 0:128], in0=ident[:, :],
                                                            scalar1=tm["gc"][:, n, j:j + 1], scalar2=None, op0=ALU.mult),
                 [ident_r, tm_r["gc"]], [Ct_r["rhsb"]])
            yield
            S.op("dve", lambda e, n=n, j=j: e.tensor_scalar(out=Ct["rhsb"][:, 128:256], in0=ident[:, :],
                                                            scalar1=tm["sqb"][:, n, j:j + 1], scalar2=None, op0=ALU.mult),
                 [ident_r, tm_r["sqb"]], [Ct_r["rhsb"]])
            yield
            S.op("pe", lambda e: e.matmul(c.ps[pbk][:, 0:256], ones[:, :], Ct["rhsb"][:, :], start=True, stop=True),
                 [ones_r, Ct_r["rhsb"]], [qs_r[pbk]])
            yield
            S.op("act", lambda e: e.activation(out=Ct["egc"][:, :], in_=c.ps[pbk][:, 0:128], func=AF.Exp),
                 [qs_r[pbk]], [Ct_r["egc"]])
            yield
            S.op("act", lambda e, n=n, j=j: e.activation(out=Ct["rl"][:, :], in_=c.ps[pbk][:, 0:128], func=AF.Relu,
                                                         scale=1.0, bias=tm["ngc"][:, n, j:j + 1]),
                 [qs_r[pbk], tm_r["ngc"]], [Ct_r["rl"]])
            yield
            S.op("act", lambda e: e.activation(out=Ct["decs"][:, :], in_=Ct["rl"][:, :], func=AF.Exp, scale=-1.0),
                 [Ct_r["rl"]], [Ct_r["decs"]])
            yield
            S.op("act", lambda e, n=n, j=j: e.activation(out=Ct["rl"][:, :], in_=c.ps[pbk][:, 0:128], func=AF.Relu,
                                                         scale=-1.0, bias=tm["gc"][:, n, j:j + 1]),
                 [qs_r[pbk], tm_r["gc"]], [Ct_r["rl"]])
            yield
            S.op("act", lambda e: e.activation(out=Ct["dTi"][:, :], in_=Ct["rl"][:, :], func=AF.Exp, scale=-1.0),
                 [Ct_r["rl"]], [Ct_r["dTi"]])
            yield
            S.op("dve", lambda e: e.tensor_tensor(out=Ct["decs"][:, :], in0=Ct["decs"][:, :], in1=mls[:, :], op=ALU.mult),
                 [mls_r], [Ct_r["decs"]])
            yield
            S.op("dve", lambda e: e.tensor_tensor(out=Ct["dTs"][:, :], in0=Ct["dTi"][:, :], in1=mus[:, :], op=ALU.mult),
                 [mus_r, Ct_r["dTi"]], [Ct_r["dTs"]])
            yield
            S.op("dve", lambda e: e.tensor_tensor(out=Ct["dTi"][:, :], in0=Ct["dTi"][:, :], in1=mui[:, :], op=ALU.mult),
                 [mui_r, Ct_r["dTs"]], [Ct_r["dTi"]])
            yield
            S.op("dve", lambda e, csl=csl: e.tensor_tensor(out=Ct["ktT"][:, :], in0=F["knT"][:, csl], in1=c.ps[pbk][:, 128:256],
                                                           op=ALU.mult), [F_r["knT"], qs_r[pbk]], [Ct_r["ktT"]])
            yield
            S.op("dve", lambda e, csl=csl: e.tensor_tensor(out=Ct["qdT"][:, :], in0=F["qf"][:, csl], in1=Ct["egc"][:, :],
                                                           op=ALU.mult), [F_r["qf"], Ct_r["egc"]], [Ct_r["qdT"]])
            yield
            g_bq = pslot()
            mm(g_bq, Ct["ktT"][:, :], Ct["ktT"][:, :], [Ct_r["ktT"]])
            yield
            S.op("dve", lambda e, g_bq=g_bq: e.tensor_tensor(out=Ct["A0"][:, :], in0=psap(g_bq), in1=Ct["decs"][:, :],
                                                             op=ALU.mult), [qs_r[g_bq], Ct_r["decs"]], [Ct_r["A0"]])
            yield
            S.op("dve", lambda e, g_bq=g_bq: e.tensor_tensor(out=Ct["B0"][:, :], in0=psap(g_bq), in1=Ct["dTs"][:, :],
                                                             op=ALU.mult), [qs_r[g_bq], Ct_r["dTs"]], [Ct_r["B0"]])
            yield
            a_bq = pslot()
            mm(a_bq, F["knT"][:, csl], F["qf"][:, csl], [F_r["knT"], F_r["qf"]])
            yield
            S.op("dve", lambda e, a_bq=a_bq: e.tensor_tensor(out=Ct["attnT"][:, :], in0=psap(a_bq), in1=Ct["dTi"][:, :],
                                                             op=ALU.mult), [qs_r[a_bq], Ct_r["dTi"]], [Ct_r["attnT"]])
            yield
            t_bq = pslot()
            S.op("pe", lambda e, t_bq=t_bq, csl=csl: e.transpose(out=psap(t_bq), in_=F["knT"][:, csl], identity=ident[:, :]),
                 [F_r["knT"], ident_r], [qs_r[t_bq]])
            yield
            evac(Ct["kd"][:, :], Ct_r["kd"], t_bq, [tm_r["ekd"]], scale=tm["ekd"][:, n, j:j + 1])
            yield
            t_bq = pslot()
            S.op("pe", lambda e, t_bq=t_bq, csl=csl: e.transpose(out=psap(t_bq), in_=F["vf"][:, csl], identity=ident[:, :]),
                 [F_r["vf"], ident_r], [qs_r[t_bq]])
            yield
            evac(Ct["vs"][:, :], Ct_r["vs"], t_bq, [tm_r["sqb"]], scale=tm["sqb"][:, n, j:j + 1])
            yield
            S.op("dve", lambda e: e.tensor_tensor(out=Ct["P"][:, :], in0=ident[:, :], in1=Ct["B0"][:, :], op=ALU.subtract),
                 [ident_r, Ct_r["B0"]], [Ct_r["P"]])
            yield
            Ak, Bk = "A0", "B0"
            for lvl in range(1, 7):
                An, Bn = ("A1", "B1") if Ak == "A0" else ("A0", "B0")
                bq = pslot()
                mm(bq, Ct[Bk][:, :], Ct[Ak][:, :], [Ct_r[Bk], Ct_r[Ak]])
                yield
                if lvl < 6:
                    bq2 = pslot()
                    mm(bq2, Ct[Ak][:, :], Ct[Bk][:, :], [Ct_r[Bk], Ct_r[Ak]])
                    yield
                evac(Ct[An][:, :], Ct_r[An], bq)
                yield
                if lvl < 6:
                    evac(Ct[Bn][:, :], Ct_r[Bn], bq2)
                    yield
                bq3 = pslot()
                mm(bq3, Ct[An][:, :], Ct["P"][:, :], [Ct_r[An], Ct_r["P"]])
                yield
                S.op("dve", lambda e, bq3=bq3: e.tensor_tensor(out=Ct["P"][:, :], in0=Ct["P"][:, :], in1=psap(bq3),
                                                               op=ALU.add), [qs_r[bq3]], [Ct_r["P"]])
                yield
                Ak, Bk = An, Bn


        def recur(n, bset):
            Ct, Ct_r = Cts[bset]
            csl = slice(n * 128, (n + 1) * 128)
            bq = pslot()
            mm(bq, F["knT"][:, csl], Sj, [F_r["knT"], Sj_r])
            yield
            S.op("dve", lambda e, bq=bq, n=n, j=j: e.scalar_tensor_tensor(
                out=Ct["Rp"][:, :], in0=psap(bq), scalar=tm["nbe"][:, n, j:j + 1], in1=Ct["vs"][:, :], op0=ALU.mult,
                op1=ALU.add), [qs_r[bq], tm_r["nbe"], Ct_r["vs"]], [Ct_r["Rp"]])
            yield
            bq = pslot()
            mm(bq, Ct["P"][:, :], Ct["Rp"][:, :], [Ct_r["P"], Ct_r["Rp"]])
            yield
            evac(Ct["vn"][:, :], Ct_r["vn"], bq, [tm_r["sqb"]], scale=tm["sqb"][:, n, j:j + 1])
            yield
            o_bq = pslot()
            mm(o_bq, Sj, Ct["qdT"][:, :], [Sj_r, Ct_r["qdT"]], start=True, stop=False)
            mm(o_bq, Ct["vn"][:, :], Ct["attnT"][:, :], [Ct_r["vn"], Ct_r["attnT"]], start=False, stop=True)
            yield
            evac(F["oT"][:, csl], F_r["oT"], o_bq)
            yield
            bq = pslot()
            mm(bq, Ct["kd"][:, :], Ct["vn"][:, :], [Ct_r["kd"], Ct_r["vn"]])
            yield
            S.op("dve", lambda e, bq=bq, n=n, j=j, Sj=Sj: e.scalar_tensor_tensor(
                out=Sj, in0=Sj, scalar=tm["egl"][:, n, j:j + 1], in1=psap(bq), op0=ALU.mult, op1=ALU.add),
                [qs_r[bq], tm_r["egl"]], [Sj_r])
            yield


        def interleave(gens):
            gens = [g for g in gens if g is not None]
            while gens:
                for g in list(gens):
                    try:
                        next(g)
                    except StopIteration:
                        gens.remove(g)

        def chain(*gs):
            for g in gs:
                yield from g

        interleave([prep(0, 0), prep(1, 1)])
        for m in range(NCH // 2):
            nxt = []
            if 2 * m + 2 < NCH:
                nxt = [prep(2 * m + 2, (2 * m + 2) % 4), prep(2 * m + 3, (2 * m + 3) % 4)]
            interleave([chain(recur(2 * m, (2 * m) % 4), recur(2 * m + 1, (2 * m + 1) % 4))] + nxt)
        S.op("act", lambda e: e.activation(out=F["t1"][:, :], in_=F["oT"][:, :], func=AF.Square), [F_r["oT"], F_r["t1"]],
             [F_r["t1"]])
        for half in range(2):
            S.op("pe", lambda e, half=half: e.matmul(c.ps[half][:, :], ones[:, :], F["t1"][:, half * 512:(half + 1) * 512],
                                                     start=True, stop=True), [ones_r, F_r["t1"]], [c.psr[half]])
        for half in range(2):
            S.op("dve", lambda e, half=half: e.scalar_tensor_tensor(
                out=F["t1"][:, half * 512:(half + 1) * 512], in0=c.ps[half][:, :], scalar=1.0 / 128.0,
                in1=F["epsq"][:, half * 512:(half + 1) * 512], op0=ALU.mult, op1=ALU.add),
                [c.psr[half], F_r["epsq"], F_r["t1"]], [F_r["t1"]])
        S.op("act", lambda e: e.activation(out=F["t1"][:, :], in_=F["t1"][:, :], func=AF.Ln), [F_r["t1"]], [F_r["t1"]])
        S.op("act", lambda e: e.activation(out=F["t1"][:, :], in_=F["t1"][:, :], func=AF.Exp, scale=-0.5),
             [F_r["t1"]], [F_r["t1"]])
        S.op("dve", lambda e: e.tensor_tensor(out=F["t1"][:, :], in0=F["t1"][:, :], in1=F["oT"][:, :], op=ALU.mult),
             [F_r["t1"], F_r["oT"]], [F_r["t1"]])
        S.op("act", lambda e: e.activation(out=F["oT"][:, :], in_=raw["z"][:, HALO:NT], func=AF.Silu),
             [raw_r["z"], F_r["t1"]], [F_r["oT"]])
        yb = j % 2
        S.op("dve", lambda e, yb=yb: e.scalar_tensor_tensor(out=yst[yb][:, :], in0=F["t1"][:, :], scalar=dnw[:, 0:1],
                                                            in1=F["oT"][:, :], op0=ALU.mult, op1=ALU.mult),
             [F_r["t1"], F_r["oT"], dnw_r], [yst_r[yb]])
        yr = Res("yscr")
        S.dma("sp", yscr[DL + j * 128:DL + (j + 1) * 128, :], yst[yb][:, :], [yst_r[yb]], [yr], yst_r[yb])
        yscr_rs.append(yr)


def phase_a_inputs(inp, b):
    f = lambda a: np.ascontiguousarray(a, dtype=np.float32)
    x = inp["x"]
    xe = np.zeros((NSEG * NTOK + HALO, D), np.float32)
    xe[HALO:] = x[b]
    lcw = np.stack([fm_vec(inp["lru_conv_w"][0, k]) for k in range(4)], axis=-1).reshape(128, NH * 4)
    dcw = np.stack([fm_vec(inp["dn_conv_w"][0, k]) for k in range(4)], axis=-1).reshape(128, 48 * 4)
    idx = np.arange(128)
    return {
        "xe": xe, "nwfm": fm_vec(inp["even_norm_w"][0]), "w_in": f(inp["even_w_in"][0]), "w_out": f(inp["even_w_out"][0]),
        "lcw": f(lcw), "lcb": fm_vec(inp["lru_conv_b"][0]), "lbr": fm_vec(inp["lru_b_r"][0]),
        "lbi": fm_vec(inp["lru_b_i"][0]), "lam": fm_vec(inp["lru_lambda"][0]),
        "wr": f(inp["lru_w_r"][0]), "wi": f(inp["lru_w_i"][0]), "dcw": f(dcw),
        "alog_b": f(np.broadcast_to(inp["dn_a_log"][0], (128, NH))), "dtb_b": f(np.broadcast_to(inp["dn_dt_bias"][0], (128, NH))),
        "dnw": f(inp["dn_norm_w"][0].reshape(128, 1)),
        "ident": np.eye(128, dtype=np.float32), "ones": np.ones((128, 128), np.float32),
        "ut": f(idx[:, None] <= idx[None, :]), "mls": f(idx[None, :] < idx[:, None]),
        "mus": f(idx[None, :] > idx[:, None]), "mui": f(idx[None, :] >= idx[:, None]),
    }


_CACHE = {}


def build_fused():
    nc = bass.Bass("TRN2", target_bir_lowering=False)
    TA = phase_a_decl(nc, "a_")
    TC = phase_c_decl(nc, "c_", with_x1e=False)
    msel_d = nc.dram_tensor("msel", [128, 4], F32, kind="ExternalInput").ap()
    zrow_d = nc.dram_tensor("zrow", [HALO, D], F32, kind="ExternalInput").ap()
    x1full = nc.dram_tensor("x1full", [NSEG * NTOK + HALO, D], F32, kind="Internal").ap()
    xown = nc.dram_tensor("xown", [NT, D], F32, kind="Internal").ap()
    TA["x1"] = x1full[HALO:, :]
    es = ExitStack()
    with es:
        c = setup_common(nc, es)
        S = c.S
        with ExitStack() as sa:
            zt = sa.enter_context(nc.sbuf_tensor("zt", [HALO, D], F32))
            zt_r = Res("zt")
            S.dma("sp", zt[:, :], zrow_d, [], [zt_r], zt_r)
            zr = Res("zrows")
            S.dma("sp", x1full[0:HALO, :], zt[:, :], [zt_r], [zr], zt_r)
            phase_a_body(c, sa, nc, TA)
            S.barrier()
        S.new_generation()
        with ExitStack() as sb:
            phase_c_body(c, sb, nc, TC, sel=(x1full, msel_d, xown))
        S.finalize()
    return nc


def kernel(x, even_norm_w, even_w_in, lru_conv_w, lru_conv_b, lru_w_r, lru_b_r, lru_w_i, lru_b_i, lru_lambda,
           dn_conv_w, dn_a_log, dn_dt_bias, dn_norm_w, even_w_out, odd_norm_w, odd_w_in, odd_conv_w, odd_w_out,
           final_norm_w):
    A = np.asarray
    inp = dict(x=A(x, np.float32), even_norm_w=A(even_norm_w), even_w_in=A(even_w_in),
               lru_conv_w=A(lru_conv_w), lru_conv_b=A(lru_conv_b), lru_w_r=A(lru_w_r),
               lru_b_r=A(lru_b_r), lru_w_i=A(lru_w_i), lru_b_i=A(lru_b_i),
               lru_lambda=A(lru_lambda), dn_conv_w=A(dn_conv_w), dn_a_log=A(dn_a_log),
               dn_dt_bias=A(dn_dt_bias), dn_norm_w=A(dn_norm_w), even_w_out=A(even_w_out))
    if "f" not in _CACHE:
        _CACHE["f"] = build_fused()
    odd_conv_w = A(odd_conv_w)
    cwl = np.ascontiguousarray(np.stack([fm_vec(odd_conv_w[0, k]) for k in range(3)], axis=-1).reshape(128, KC * 3))
    cmn = {
        "c_nwfm": fm_vec(A(odd_norm_w)[0]), "c_w_in": np.ascontiguousarray(A(odd_w_in)[0], dtype=np.float32), "c_cw": cwl,
        "c_w_out": np.ascontiguousarray(A(odd_w_out)[0], dtype=np.float32),
        "c_fnw": np.ascontiguousarray(np.broadcast_to(A(final_norm_w).astype(np.float32), (128, D))),
        "c_ident": np.eye(128, dtype=np.float32), "zrow": np.zeros((HALO, D), np.float32),
    }
    pa = [{"a_" + k: v for k, v in phase_a_inputs(inp, b).items()} for b in range(2)]
    in_maps = []
    for core in range(8):
        b, sq = divmod(core, 4)
        m = dict(cmn)
        m.update(pa[b])
        ms = np.zeros((128, 4), np.float32)
        ms[:, sq] = 1.0
        m["msel"] = ms
        in_maps.append(m)
    res = run_bass_kernel_spmd(_CACHE["f"], in_maps, core_ids=list(range(8)))
    out = np.zeros((2, NSEG * NTOK, D), np.float32)
    for core in range(8):
        b, sq = divmod(core, 4)
        out[b, sq * NTOK:(sq + 1) * NTOK] = res.results[core]["out"]
    return out
```
